# Optimizing a Trainium2 kernel written in Bass

```python
import jax, jax.numpy as jnp
from jax import lax
import numpy as np

D_MODEL = 2048
BATCH = 4
SEQ = 4096
DEPTH = 2

N_A_LAYERS = DEPTH // 2
N_B_LAYERS = DEPTH - N_A_LAYERS
N_MEM = 256
MEM_HEADS = 4
MEM_HEAD_DIM = D_MODEL // 16
MEM_WIDTH = MEM_HEADS * MEM_HEAD_DIM
BRANCH_WIDTH = D_MODEL - MEM_WIDTH
GLA_HEADS = 4
GLA_DV = BRANCH_WIDTH // GLA_HEADS
GLA_DK = GLA_DV // 2
GLA_GATE_RANK = 16
GLA_GATE_NORM = 16.0
GLA_CHUNK = 64
MLA_HEADS = 12
MLA_V_DIM = BRANCH_WIDTH // MLA_HEADS
MLA_NOPE_DIM = 128
MLA_ROPE_DIM = 64
MLA_Q_RANK = D_MODEL // 4
MLA_KV_RANK = D_MODEL // 4
ROPE_THETA = 10000.0
Q_BLOCK = 128
EPS = 1e-6

A_SPLITS = [GLA_HEADS * GLA_DK, GLA_HEADS * GLA_DK, BRANCH_WIDTH, GLA_GATE_RANK,
            BRANCH_WIDTH, MEM_WIDTH, MEM_WIDTH]
A_IN_WIDTH = sum(A_SPLITS)
B_SPLITS = [MLA_Q_RANK, BRANCH_WIDTH, MEM_WIDTH, MEM_WIDTH]
B_IN_WIDTH = sum(B_SPLITS)
MLA_QK_DIM = MLA_NOPE_DIM + MLA_ROPE_DIM

kernel_name = 'yoco_gla_mla_memory_hybrid'


def _split(t, sizes):
    return jnp.split(t, list(np.cumsum(sizes)[:-1]), axis=-1)


def rmsnorm(x, g):
    xf = x.astype(jnp.float32)
    xf = xf * lax.rsqrt(jnp.mean(xf * xf, axis=-1, keepdims=True) + EPS)
    return (xf * g.astype(jnp.float32)).astype(x.dtype)


def rope(x, positions):
    r = x.shape[-1]
    freqs = ROPE_THETA ** (-jnp.arange(0, r, 2, dtype=jnp.float32) / r)
    ang = positions.astype(jnp.float32)[..., None] * freqs
    ang = ang.reshape(ang.shape[:2] + (1,) * (x.ndim - 3) + (r // 2,))
    cos, sin = jnp.cos(ang), jnp.sin(ang)
    xf = x.astype(jnp.float32)
    x1, x2 = xf[..., : r // 2], xf[..., r // 2:]
    return jnp.concatenate([x1 * cos - x2 * sin, x2 * cos + x1 * sin], axis=-1).astype(x.dtype)


def mem_attend(q, mem, mem_g, w_kv):
    b_, s_ = q.shape[:2]
    m_ = mem.shape[1]
    k, v = jnp.split(rmsnorm(mem, mem_g) @ w_kv, 2, axis=-1)
    k = k.reshape(b_, m_, MEM_HEADS, MEM_HEAD_DIM)
    v = v.reshape(b_, m_, MEM_HEADS, MEM_HEAD_DIM)
    qh = q.reshape(b_, s_, MEM_HEADS, MEM_HEAD_DIM)
    s = jnp.einsum('bshd,bmhd->bhsm', qh, k).astype(jnp.float32) * MEM_HEAD_DIM ** -0.5
    p = jax.nn.softmax(s, axis=-1).astype(v.dtype)
    return jnp.einsum('bhsm,bmhd->bshd', p, v).reshape(b_, s_, MEM_WIDTH)


def gla_chunked(q, k, v, log_a):
    b_, s_ = q.shape[:2]
    nc = s_ // GLA_CHUNK

    def to_chunks(t):
        return t.astype(jnp.float32).reshape(b_, nc, GLA_CHUNK, GLA_HEADS, t.shape[-1]).transpose(1, 0, 3, 2, 4)

    qc, kc, vc, gc = to_chunks(q), to_chunks(k), to_chunks(v), to_chunks(log_a)
    causal = jnp.tril(jnp.ones((GLA_CHUNK, GLA_CHUNK), dtype=bool))[None, None, :, :, None]

    def step(state, inp):
        qi, ki, vi, gi = inp
        cum = jnp.cumsum(gi, axis=2)
        o_inter = jnp.einsum('bhck,bhkv->bhcv', qi * jnp.exp(cum), state)
        diff = cum[:, :, :, None, :] - cum[:, :, None, :, :]
        decay = jnp.where(causal, jnp.exp(jnp.where(causal, diff, 0.0)), 0.0)
        attn = jnp.einsum('bhtk,bhsk,bhtsk->bhts', qi, ki, decay)
        o_intra = jnp.einsum('bhts,bhsv->bhtv', attn, vi)
        last = cum[:, :, -1:, :]
        state = jnp.exp(last[:, :, 0, :])[..., None] * state + jnp.einsum(
            'bhsk,bhsv->bhkv', ki * jnp.exp(last - cum), vi)
        return state, o_inter + o_intra

    s0 = jnp.zeros((b_, GLA_HEADS, GLA_DK, GLA_DV), jnp.float32)
    _, o = lax.scan(step, s0, (qc, kc, vc, gc))
    return o.transpose(1, 0, 3, 2, 4).reshape(b_, s_, GLA_HEADS, GLA_DV).astype(v.dtype)


def layer_a(x, mem, pre_g, w_in, w_g2, b_g, gla_g, mem_g, w_mem_kv, w_out, post_g):
    b_, s_ = x.shape[:2]
    h = rmsnorm(x, pre_g)
    q, k, v, g_lr, z, mq, mz = _split(h @ w_in, A_SPLITS)
    q = q.reshape(b_, s_, GLA_HEADS, GLA_DK) * GLA_DK ** -0.5
    k = k.reshape(b_, s_, GLA_HEADS, GLA_DK)
    v = v.reshape(b_, s_, GLA_HEADS, GLA_DV)
    log_a = jax.nn.log_sigmoid((g_lr @ w_g2 + b_g).astype(jnp.float32)) / GLA_GATE_NORM
    log_a = log_a.reshape(b_, s_, GLA_HEADS, GLA_DK)
    o = rmsnorm(gla_chunked(q, k, v, log_a), gla_g)
    gla_out = o.reshape(b_, s_, BRANCH_WIDTH) * jax.nn.silu(z)
    mem_out = mem_attend(mq, mem, mem_g, w_mem_kv) * jax.nn.silu(mz)
    y = jnp.concatenate([gla_out, mem_out], axis=-1) @ w_out
    return x + rmsnorm(y, post_g)


def shared_mla_kv(x, positions, kv_in_g, w_dkv, kv_g, w_uk, w_uv):
    b_, s_ = x.shape[:2]
    c, k_rope = _split(rmsnorm(x, kv_in_g) @ w_dkv, [MLA_KV_RANK, MLA_ROPE_DIM])
    c = rmsnorm(c, kv_g)
    k_nope = (c @ w_uk).reshape(b_, s_, MLA_HEADS, MLA_NOPE_DIM)
    v = (c @ w_uv).reshape(b_, s_, MLA_HEADS, MLA_V_DIM)
    k_rope = rope(k_rope, positions)
    return k_nope, k_rope, v


def mla_attention(q_nope, q_rope, k_nope, k_rope, v):
    b_, s_ = q_nope.shape[:2]
    nb = s_ // Q_BLOCK
    qn = q_nope.reshape(b_, nb, Q_BLOCK, MLA_HEADS, MLA_NOPE_DIM).transpose(1, 0, 3, 2, 4)
    qr = q_rope.reshape(b_, nb, Q_BLOCK, MLA_HEADS, MLA_ROPE_DIM).transpose(1, 0, 3, 2, 4)
    kpos = jnp.arange(s_)

    def block(args):
        qn_b, qr_b, start = args
        s = jnp.einsum('bhqd,bkhd->bhqk', qn_b, k_nope) + jnp.einsum('bhqr,bkr->bhqk', qr_b, k_rope)
        s = s.astype(jnp.float32) * MLA_QK_DIM ** -0.5
        qpos = start + jnp.arange(Q_BLOCK)
        s = jnp.where(kpos[None, :] <= qpos[:, None], s, -jnp.inf)
        p = jax.nn.softmax(s, axis=-1).astype(v.dtype)
        return jnp.einsum('bhqk,bkhd->bqhd', p, v)

    starts = jnp.arange(nb, dtype=jnp.int32) * Q_BLOCK
    o = lax.map(block, (qn, qr, starts))
    return o.transpose(1, 0, 2, 3, 4).reshape(b_, s_, MLA_HEADS, MLA_V_DIM)


def layer_b(x, mem, positions, k_nope, k_rope, v, pre_g, w_in, q_g, w_uq, mem_g, w_mem_kv, w_out, post_g):
    b_, s_ = x.shape[:2]
    h = rmsnorm(x, pre_g)
    cq, z, mq, mz = _split(h @ w_in, B_SPLITS)
    q = (rmsnorm(cq, q_g) @ w_uq).reshape(b_, s_, MLA_HEADS, MLA_QK_DIM)
    q_nope, q_rope = q[..., :MLA_NOPE_DIM], rope(q[..., MLA_NOPE_DIM:], positions)
    o = mla_attention(q_nope, q_rope, k_nope, k_rope, v)
    mla_out = o.reshape(b_, s_, BRANCH_WIDTH) * jax.nn.silu(z)
    mem_out = mem_attend(mq, mem, mem_g, w_mem_kv) * jax.nn.silu(mz)
    y = jnp.concatenate([mla_out, mem_out], axis=-1) @ w_out
    return x + rmsnorm(y, post_g)


def setup_inputs(seed: int = 0) -> dict:
    key = jax.random.key(seed)
    ks = jax.random.split(key, 32)
    f32 = jnp.float32

    def w(k, shape, fan_in):
        return jax.random.normal(k, shape, f32) * fan_in ** -0.5

    def gain(k, shape):
        return 1.0 + 0.05 * jax.random.normal(k, shape, f32)

    na, nbl = N_A_LAYERS, N_B_LAYERS
    return {
        'x': jax.random.normal(ks[0], (BATCH, SEQ, D_MODEL), f32),
        'mem': jax.random.normal(ks[1], (BATCH, N_MEM, D_MODEL), f32),
        'positions': (jax.random.randint(ks[2], (BATCH, 1), 0, 512, dtype=jnp.int32)
                      + jnp.arange(SEQ, dtype=jnp.int32)[None, :]),
        'a_pre_norm': gain(ks[3], (na, D_MODEL)),
        'a_w_in': w(ks[4], (na, D_MODEL, A_IN_WIDTH), D_MODEL),
        'a_w_g2': w(ks[5], (na, GLA_GATE_RANK, GLA_HEADS * GLA_DK), GLA_GATE_RANK),
        'a_b_g': 0.5 + 0.1 * jax.random.normal(ks[6], (na, GLA_HEADS * GLA_DK), f32),
        'a_gla_norm': gain(ks[7], (na, GLA_DV)),
        'a_mem_norm': gain(ks[8], (na, D_MODEL)),
        'a_w_mem_kv': w(ks[9], (na, D_MODEL, 2 * MEM_WIDTH), D_MODEL),
        'a_w_out': w(ks[10], (na, D_MODEL, D_MODEL), D_MODEL),
        'a_post_norm': gain(ks[11], (na, D_MODEL)),
        'kv_in_norm': gain(ks[12], (D_MODEL,)),
        'w_dkv': w(ks[13], (D_MODEL, MLA_KV_RANK + MLA_ROPE_DIM), D_MODEL),
        'kv_norm': gain(ks[14], (MLA_KV_RANK,)),
        'w_uk': w(ks[15], (MLA_KV_RANK, MLA_HEADS * MLA_NOPE_DIM), MLA_KV_RANK),
        'w_uv': w(ks[16], (MLA_KV_RANK, MLA_HEADS * MLA_V_DIM), MLA_KV_RANK),
        'b_pre_norm': gain(ks[17], (nbl, D_MODEL)),
        'b_w_in': w(ks[18], (nbl, D_MODEL, B_IN_WIDTH), D_MODEL),
        'b_q_norm': gain(ks[19], (nbl, MLA_Q_RANK)),
        'b_w_uq': w(ks[20], (nbl, MLA_Q_RANK, MLA_HEADS * MLA_QK_DIM), MLA_Q_RANK),
        'b_mem_norm': gain(ks[21], (nbl, D_MODEL)),
        'b_w_mem_kv': w(ks[22], (nbl, D_MODEL, 2 * MEM_WIDTH), D_MODEL),
        'b_w_out': w(ks[23], (nbl, D_MODEL, D_MODEL), D_MODEL),
        'b_post_norm': gain(ks[24], (nbl, D_MODEL)),
    }


def reference(x, mem, positions, a_pre_norm, a_w_in, a_w_g2, a_b_g, a_gla_norm, a_mem_norm,
              a_w_mem_kv, a_w_out, a_post_norm, kv_in_norm, w_dkv, kv_norm, w_uk, w_uv,
              b_pre_norm, b_w_in, b_q_norm, b_w_uq, b_mem_norm, b_w_mem_kv, b_w_out, b_post_norm):
    k_nope = k_rope = v_sh = None
    for i in range(DEPTH):
        if i < N_A_LAYERS:
            x = layer_a(x, mem, a_pre_norm[i], a_w_in[i], a_w_g2[i], a_b_g[i], a_gla_norm[i],
                        a_mem_norm[i], a_w_mem_kv[i], a_w_out[i], a_post_norm[i])
        else:
            j = i - N_A_LAYERS
            if j == 0:
                k_nope, k_rope, v_sh = shared_mla_kv(x, positions, kv_in_norm, w_dkv, kv_norm, w_uk, w_uv)
            x = layer_b(x, mem, positions, k_nope, k_rope, v_sh, b_pre_norm[j], b_w_in[j], b_q_norm[j],
                        b_w_uq[j], b_mem_norm[j], b_w_mem_kv[j], b_w_out[j], b_post_norm[j])
    return x
```

```python
import numpy as np
from contextlib import ExitStack
import concourse.bass as bass
import concourse.mybir as mybir
from concourse.bass_utils import run_bass_kernel_spmd

F32 = mybir.dt.float32
BF16 = mybir.dt.bfloat16
I32 = mybir.dt.int32
AF = mybir.ActivationFunctionType
ALU = mybir.AluOpType

ENGS = ("sync", "scalar", "vector", "gpsimd", "tensor")

D = 2048
T = 4096
TO = 2048
NM = 256
EPS = 1e-6
DEBUG_OUT = []


class Buf:
    __slots__ = ("name", "w", "r", "dsem", "dcnt")

    def __init__(self, name):
        self.name = name
        self.w = None
        self.r = []
        self.dsem = None
        self.dcnt = 0


class Phase:
    def __init__(self, nc, name):
        self.nc = nc
        self.name = name
        self.stack = ExitStack()
        self.ops = {e: [] for e in ENGS}
        self.sems = []
        self.esem = {}
        self.ecnt = {}
        for e in ENGS:
            if e == "sync":
                continue
            self.esem[e] = nc.alloc_semaphore(name=f"{name}_{e}")
            self.sems.append(self.esem[e])
            self.ecnt[e] = 0
        self.emitted = {e: {} for e in ENGS}
        self.pending = {e: [] for e in ENGS}
        self.store_toks = []
        self.nbuf = 0
        self.nsem = 4
        self.rr = 0

    def sb(self, name, shape, dt):
        return self.stack.enter_context(self.nc.sbuf_tensor(f"{self.name}_{name}", list(shape), dt))

    def ps(self, name, shape, dt=F32):
        return self.stack.enter_context(self.nc.psum_tensor(f"{self.name}_{name}", list(shape), dt))

    def buf(self, name="b"):
        self.nbuf += 1
        return Buf(f"{name}{self.nbuf}")

    def bufs(self, n, name="b"):
        return [self.buf(name) for _ in range(n)]

    def _deps(self, eng, reads, writes):
        toks = []
        for b in reads:
            if b.w is not None:
                toks.append(b.w)
        for b in writes:
            if b.w is not None:
                toks.append(b.w)
            toks.extend(b.r)
        em = self.emitted[eng]
        best = {}
        for (sem, val) in toks:
            if isinstance(sem, str):
                if val != eng:
                    raise RuntimeError(f"cross-engine dep on unsignaled op ({val}->{eng}) in phase {self.name}")
                continue
            k = id(sem)
            if val > em.get(k, 0) and val > best.get(k, (None, 0))[1]:
                best[k] = (sem, val)
        waits = []
        for k, (sem, val) in best.items():
            em[k] = val
            waits.append((sem, val))
        return waits

    def _mark(self, tok, reads, writes):
        for b in reads:
            b.r.append(tok)
        for b in writes:
            b.w = tok
            b.r = []

    def op(self, eng, fn, reads=(), writes=(), signal=True):
        waits = self._deps(eng, reads, writes)
        if signal:
            self.ecnt[eng] += 1
            tok = (self.esem[eng], self.ecnt[eng])
            ptok = ("PENDING", eng)
            for (r_, w_) in self.pending[eng]:
                for b in w_:
                    if b.w == ptok:
                        b.w = tok
                for b in list(r_) + list(w_):
                    b.r = [tok if t == ptok else t for t in b.r]
            self.pending[eng] = []
            self._mark(tok, reads, writes)
            self.ops[eng].append((waits, fn, (self.esem[eng], 1)))
        else:
            ptok = ("PENDING", eng)
            self.pending[eng].append((list(reads), list(writes)))
            self._mark(ptok, reads, writes)
            self.ops[eng].append((waits, fn, None))

    def dma(self, q, out, in_, reads=(), writes=()):
        outs = out if isinstance(out, (list, tuple)) else [out]
        ins = in_ if isinstance(in_, (list, tuple)) else [in_]
        waits = self._deps(q, reads, writes)
        key = writes[0] if writes else reads[0]
        if key.dsem is None:
            key.dsem = self.nc.alloc_semaphore(name=f"{self.name}_d{self.nsem}")
            self.sems.append(key.dsem)
            self.nsem += 1
        for n, (o, i) in enumerate(zip(outs, ins)):
            key.dcnt += 16
            self.ops[q].append((waits if n == 0 else [],
                                (lambda e, o=o, i=i: e.dma_start(out=o, in_=i)), (key.dsem, 16)))
        tok = (key.dsem, key.dcnt)
        self._mark(tok, reads, writes)
        if not writes:
            self.store_toks.append(tok)
        return tok

    def evac_eng(self):
        self.rr += 1
        return "scalar" if self.rr % 2 else "vector"

    def finish(self):
        nc = self.nc
        last = {}
        for (sem, val) in self.store_toks:
            k = id(sem)
            if last.get(k, (None, 0))[1] < val:
                last[k] = (sem, val)
        final_waits = list(last.values())
        for e in ENGS:
            assert not self.pending[e], f"pending unsignaled ops at end of phase {self.name} on {e}"
        ops = self.ops
        with nc.Block(self.name) as block:
            def run(engname):
                def body(e):
                    for (waits, fn, inc) in ops[engname]:
                        for (sem, val) in waits:
                            e.wait_ge(sem, val)
                        ins = fn(e)
                        if inc is not None:
                            ins.then_inc(inc[0], inc[1])
                    if engname == "sync":
                        for (sem, val) in final_waits:
                            e.wait_ge(sem, val)
                return body
            block.sync(run("sync"))
            block.scalar(run("scalar"))
            block.vector(run("vector"))
            block.gpsimd(run("gpsimd"))
            block.tensor(run("tensor"))
        nc.clear_and_free_semaphores(self.sems)
        nc.all_engine_barrier()
        self.stack.close()


def copy_op(P, eng, out, in_, reads, writes):
    if eng == "scalar":
        P.op("scalar", lambda e: e.activation(out=out, in_=in_, func=AF.Copy), reads=reads, writes=writes)
    else:
        P.op(eng, lambda e: e.tensor_copy(out=out, in_=in_), reads=reads, writes=writes)


def phase_norm(nc, name, src, gain_d, dst, F, Tn, TT=512, sel=None):
    KC = F // 128
    P = Phase(nc, name)
    TT = min(TT, Tn)
    ones = P.sb("ones", [128, 128], BF16); b_ones = P.buf()
    P.op("vector", lambda e: e.memset(ones[:], 1.0), writes=[b_ones])
    g = P.sb("g", [128, KC], F32); b_g = P.buf()
    P.dma("sync", g[:], gain_d[:, :], writes=[b_g])
    NB = 2
    NBX = NB if sel is not None else 3
    xt = [P.sb(f"x{i}", [128, KC, TT], F32) for i in range(NBX)]; b_x = P.bufs(NBX)
    if sel is not None:
        xo = [P.sb(f"xo{i}", [128, KC, TT], F32) for i in range(NB)]; b_xo = P.bufs(NB)
        selt = P.sb("sel", [128, 2], F32); b_sel = P.buf()
        P.dma("sync", selt[:], sel[1][:, :], writes=[b_sel])
    ht = [P.sb(f"h{i}", [128, KC, TT], BF16) for i in range(NB)]; b_h = P.bufs(NB)
    sq = [P.sb(f"sq{i}", [128, TT], BF16) for i in range(3)]; b_sq = P.bufs(3)
    pss = [P.ps(f"ps{i}", [128, TT], F32) for i in range(2)]; b_ps = P.bufs(2)
    rs = [P.sb(f"rs{i}", [128, TT], F32) for i in range(2)]; b_rs = P.bufs(2)
    srcv = src.rearrange("(kc p) t -> p kc t", p=128) if sel is None else None
    dstv = dst.rearrange("(kc p) t -> p kc t", p=128)
    nsq = 0
    NT = Tn // TT

    def issue_load(it):
        t0 = it * TT
        i = it % NB
        ix = it % NBX
        x_ = xt[ix]
        if sel is None:
            P.dma("sync", x_[:], srcv[:, :, t0:t0 + TT], writes=[b_x[ix]])
        else:
            xv = sel[0].rearrange("(kc p) (i two q) -> p kc i two q", p=128, two=2, q=128)
            nb = TT // 128
            i0 = t0 // 128
            P.dma("sync", [x_[:, :, k * 128:(k + 1) * 128] for k in range(nb)], [xv[:, :, i0 + k, 0, :] for k in range(nb)], writes=[b_x[ix]])
            P.dma("sync", [xo[i][:, :, k * 128:(k + 1) * 128] for k in range(nb)], [xv[:, :, i0 + k, 1, :] for k in range(nb)], writes=[b_xo[i]])

    PF = NBX - 1
    for it in range(min(PF, NT)):
        issue_load(it)
    for it in range(NT):
        if it + PF < NT:
            issue_load(it + PF)
        t0 = it * TT
        i = it % NB
        ix = it % NBX
        x_ = xt[ix]
        if sel is not None:
            P.op("scalar", lambda e, x_=x_: e.activation(out=x_[:], in_=x_[:], func=AF.Copy, scale=selt[:, 0:1]), reads=[b_sel], writes=[b_x[ix]])
            P.op("vector", lambda e, x_=x_, xo_=xo[i]: e.scalar_tensor_tensor(out=x_[:], in0=xo_[:], scalar=selt[:, 1:2], in1=x_[:], op0=ALU.mult, op1=ALU.add),
                 reads=[b_sel, b_xo[i]], writes=[b_x[ix]])
            d32 = sel[2].rearrange("(kc p) t -> p kc t", p=128)
            P.dma("sync", d32[:, :, t0:t0 + TT], x_[:], reads=[b_x[ix]])
        ps = pss[it % 2]; bps = b_ps[it % 2]
        for kc in range(KC):
            j = nsq % 3; nsq += 1
            P.op("scalar", lambda e, kc=kc, j=j, x_=x_: e.activation(out=sq[j][:], in_=x_[:, kc, :], func=AF.Square),
                 reads=[b_x[ix]], writes=[b_sq[j]])
            P.op("tensor", lambda e, kc=kc, j=j, ps=ps: e.matmul(ps[:, :], lhsT=ones[:, :], rhs=sq[j][:], start=(kc == 0), stop=(kc == KC - 1)),
                 reads=[b_ones, b_sq[j]], writes=[bps], signal=True)
        r_ = rs[it % 2]; br = b_rs[it % 2]
        P.op("vector", lambda e, ps=ps, r_=r_: e.tensor_scalar(out=r_[:], in0=ps[:, :], scalar1=1.0 / F, scalar2=EPS, op0=ALU.mult, op1=ALU.add),
             reads=[bps], writes=[br])
        P.op("scalar", lambda e, r_=r_: e.activation(out=r_[:], in_=r_[:], func=AF.Sqrt), reads=[], writes=[br])
        P.op("vector", lambda e, r_=r_: e.reciprocal(out=r_[:], in_=r_[:]), reads=[], writes=[br])
        h_ = ht[i]
        for kc in range(KC):
            P.op("vector", lambda e, kc=kc, x_=x_, h_=h_, r_=r_: e.scalar_tensor_tensor(out=h_[:, kc, :], in0=x_[:, kc, :], scalar=g[:, kc:kc + 1], in1=r_[:], op0=ALU.mult, op1=ALU.mult),
                 reads=[b_x[ix], b_g, br], writes=[b_h[i]])
        P.dma("sync", dstv[:, :, t0:t0 + TT], h_[:], reads=[b_h[i]])
    P.finish()


def phase_normproj(nc, name, src, gain_d, F, Tn, jobs, TT=512):
    KC = F // 128
    P = Phase(nc, name)
    TT = min(TT, Tn)
    ones = P.sb("ones", [128, 128], BF16); b_ones = P.buf()
    P.op("vector", lambda e: e.memset(ones[:], 1.0), writes=[b_ones])
    g = P.sb("g", [128, KC], F32); b_g = P.buf()
    P.dma("sync", g[:], gain_d[:, :], writes=[b_g])
    wjobs = []
    for ji, (W, chunks) in enumerate(jobs):
        Wv = W.rearrange("(kc p) n -> p kc n", p=128)
        ncol = 0
        for ch in chunks:
            ch["n"] = sum(c1 - c0 for c0, c1 in ch["ranges"])
            ch["off"] = ncol
            ncol += ch["n"]
        w_ = P.sb(f"w{ji}", [128, KC, ncol], BF16); bw = P.buf()
        merged = []
        for ch in chunks:
            for (c0, c1) in ch["ranges"]:
                if merged and merged[-1][1] == c0:
                    merged[-1][1] = c1
                else:
                    merged.append([c0, c1])
        outs, ins, off = [], [], 0
        for (c0, c1) in merged:
            outs.append(w_[:, :, off:off + (c1 - c0)]); ins.append(Wv[:, :, c0:c1]); off += c1 - c0
        P.dma("gpsimd", outs, ins, writes=[bw])
        wjobs.append((w_, bw, chunks))
    NBX = 3 if KC * TT * 4 <= 32768 else 2
    xt = [P.sb(f"x{i}", [128, KC, TT], F32) for i in range(NBX)]; b_x = P.bufs(NBX)
    ht = [P.sb(f"h{i}", [128, KC, TT], BF16) for i in range(2)]; b_h = P.bufs(2)
    sq = [P.sb(f"sq{i}", [128, TT], BF16) for i in range(3)]; b_sq = P.bufs(3)
    pss = [P.ps(f"ps{i}", [128, TT], F32) for i in range(2)]; b_ps = P.bufs(2)
    rs = [P.sb(f"rs{i}", [128, TT], F32) for i in range(2)]; b_rs = P.bufs(2)
    NPP = 5
    ppj = [P.ps(f"pp{i}", [128, 512], F32) for i in range(NPP)]; b_pp = P.bufs(NPP)
    NST = 4
    st32 = [P.sb(f"s32_{i}", [128, 512], F32) for i in range(NST)]; b_s32 = P.bufs(NST)
    st16 = [P.sb(f"s16_{i}", [128, 512], BF16) for i in range(NST)]; b_s16 = P.bufs(NST)
    srcv = src.rearrange("(kc p) t -> p kc t", p=128)
    NT = Tn // TT
    cnt = dict(nsq=0, npp=0, nst=0)

    def issue_load(it):
        P.dma("sync", xt[it % NBX][:], srcv[:, :, it * TT:(it + 1) * TT], writes=[b_x[it % NBX]])

    PF = NBX - 1
    for it in range(min(PF, NT)):
        issue_load(it)
    for it in range(NT):
        if it + PF < NT:
            issue_load(it + PF)
        t0 = it * TT
        i = it % 2
        ix = it % NBX
        x_ = xt[ix]
        ps = pss[it % 2]; bps = b_ps[it % 2]
        for kc in range(KC):
            j = cnt["nsq"] % 3; cnt["nsq"] += 1
            P.op("scalar", lambda e, kc=kc, j=j, x_=x_: e.activation(out=sq[j][:], in_=x_[:, kc, :], func=AF.Square),
                 reads=[b_x[ix]], writes=[b_sq[j]])
            P.op("tensor", lambda e, kc=kc, j=j, ps=ps: e.matmul(ps[:, :], lhsT=ones[:, :], rhs=sq[j][:], start=(kc == 0), stop=(kc == KC - 1)),
                 reads=[b_ones, b_sq[j]], writes=[bps], signal=True)
        r_ = rs[it % 2]; br = b_rs[it % 2]
        P.op("vector", lambda e, ps=ps, r_=r_: e.tensor_scalar(out=r_[:], in0=ps[:, :], scalar1=1.0 / F, scalar2=EPS, op0=ALU.mult, op1=ALU.add),
             reads=[bps], writes=[br])
        P.op("scalar", lambda e, r_=r_: e.activation(out=r_[:], in_=r_[:], func=AF.Sqrt), reads=[], writes=[br])
        P.op("vector", lambda e, r_=r_: e.reciprocal(out=r_[:], in_=r_[:]), reads=[], writes=[br])
        h_ = ht[i]
        for kc in range(KC):
            P.op("vector", lambda e, kc=kc, x_=x_, h_=h_, r_=r_: e.scalar_tensor_tensor(out=h_[:, kc, :], in0=x_[:, kc, :], scalar=g[:, kc:kc + 1], in1=r_[:], op0=ALU.mult, op1=ALU.mult),
                 reads=[b_x[ix], b_g, br], writes=[b_h[i]])
        for (w_, bw, chunks) in wjobs:
            for ch in chunks:
                n, o = ch["n"], ch["off"]
                subs = [(0, TT)] if ch["kind"] == "fm" else [(k * 128, 128) for k in range(TT // 128)]
                for (s0, sw) in subs:
                    pp = ppj[cnt["npp"] % NPP]; bpp = b_pp[cnt["npp"] % NPP]; cnt["npp"] += 1
                    for kc in range(KC):
                        if ch["kind"] == "fm":
                            P.op("tensor", lambda e, pp=pp, w_=w_, kc=kc, o=o, n=n, h_=h_: e.matmul(
                                pp[0:n, 0:TT], lhsT=w_[:, kc, o:o + n], rhs=h_[:, kc, :], start=(kc == 0), stop=(kc == KC - 1)),
                                reads=[bw, b_h[i]], writes=[bpp], signal=(kc == KC - 1))
                        else:
                            P.op("tensor", lambda e, pp=pp, w_=w_, kc=kc, o=o, n=n, h_=h_, s0=s0: e.matmul(
                                pp[:, 0:n], lhsT=h_[:, kc, s0:s0 + 128], rhs=w_[:, kc, o:o + n], start=(kc == 0), stop=(kc == KC - 1)),
                                reads=[bw, b_h[i]], writes=[bpp], signal=(kc == KC - 1))
                    k = cnt["nst"] % NST; cnt["nst"] += 1
                    if ch["dt"] == F32:
                        st, bst = st32[k], b_s32[k]
                    else:
                        st, bst = st16[k], b_s16[k]
                    if ch["kind"] == "fm":
                        copy_op(P, P.evac_eng(), st[0:n, 0:TT], pp[0:n, 0:TT], [bpp], [bst])
                        P.dma("sync", ch["dst"][:, t0:t0 + TT], st[0:n, 0:TT], reads=[bst])
                    else:
                        copy_op(P, P.evac_eng(), st[:, 0:n], pp[:, 0:n], [bpp], [bst])
                        P.dma("sync", ch["dst"][t0 + s0:t0 + s0 + 128, :], st[:, 0:n], reads=[bst])
    P.finish()


def phase_proj(nc, name, srcT, W, K, Tn, chunks, fm_tt=512):
    KC = K // 128
    P = Phase(nc, name)
    fm_tt = min(fm_tt, Tn)
    hT = P.sb("hT", [128, KC, Tn], BF16)
    srcv = srcT.rearrange("(kc p) t -> p kc t", p=128)
    npiece = max(1, Tn // 512)
    pw = Tn // npiece
    b_hTs = P.bufs(npiece)
    for i in range(npiece):
        P.dma("sync", hT[:, :, i * pw:(i + 1) * pw], srcv[:, :, i * pw:(i + 1) * pw], writes=[b_hTs[i]])
    Wv = W.rearrange("(kc p) n -> p kc n", p=128)
    blocks = []
    cur = None
    for ch in chunks:
        n = sum(c1 - c0 for c0, c1 in ch["ranges"])
        ch["n"] = n
        if cur is None or cur["kind"] != ch["kind"] or cur["n"] + n > 512 or ch["kind"] == "tm":
            cur = dict(kind=ch["kind"], n=0, chunks=[])
            blocks.append(cur)
        ch["off"] = cur["n"]
        cur["n"] += n
        cur["chunks"].append(ch)
    NW = 2
    wsb = [P.sb(f"w{i}", [128, KC, 512], BF16) for i in range(NW)]; b_w = P.bufs(NW)
    NPS = 6
    pss = [P.ps(f"ps{i}", [128, 512], F32) for i in range(NPS)]; b_ps = P.bufs(NPS)
    NST = 4
    st32 = [P.sb(f"s32_{i}", [128, 512], F32) for i in range(NST)]; b_s32 = P.bufs(NST)
    st16 = [P.sb(f"s16_{i}", [128, 512], BF16) for i in range(NST)]; b_s16 = P.bufs(NST)
    nps = 0
    nst = 0
    for bi, blk in enumerate(blocks):
        w_ = wsb[bi % NW]; bw = b_w[bi % NW]
        outs, ins = [], []
        off = 0
        merged = []
        for ch in blk["chunks"]:
            for (c0, c1) in ch["ranges"]:
                if merged and merged[-1][1] == c0:
                    merged[-1][1] = c1
                else:
                    merged.append([c0, c1])
        for (c0, c1) in merged:
            outs.append(w_[:, :, off:off + (c1 - c0)])
            ins.append(Wv[:, :, c0:c1])
            off += c1 - c0
        P.dma("gpsimd", outs, ins, writes=[bw])
        if blk["kind"] == "fm":
            for it in range(Tn // fm_tt):
                t0 = it * fm_tt
                for ch in blk["chunks"]:
                    n, o = ch["n"], ch["off"]
                    ps = pss[nps % NPS]; bps = b_ps[nps % NPS]; nps += 1
                    for kc in range(KC):
                        P.op("tensor", lambda e, ps=ps, w_=w_, kc=kc, o=o, n=n, t0=t0: e.matmul(
                            ps[0:n, 0:fm_tt], lhsT=w_[:, kc, o:o + n], rhs=hT[:, kc, t0:t0 + fm_tt],
                            start=(kc == 0), stop=(kc == KC - 1)),
                            reads=[bw, b_hTs[t0 // pw]], writes=[bps], signal=(kc == KC - 1))
                    k = nst % NST; nst += 1
                    if ch["dt"] == F32:
                        st, bst = st32[k], b_s32[k]
                    else:
                        st, bst = st16[k], b_s16[k]
                    copy_op(P, P.evac_eng(), st[0:n, 0:fm_tt], ps[0:n, 0:fm_tt], [bps], [bst])
                    P.dma("sync", ch["dst"][:, t0:t0 + fm_tt], st[0:n, 0:fm_tt], reads=[bst])
        else:
            ch = blk["chunks"][0]
            n = ch["n"]
            for it in range(Tn // 128):
                t0 = it * 128
                ps = pss[nps % NPS]; bps = b_ps[nps % NPS]; nps += 1
                for kc in range(KC):
                    P.op("tensor", lambda e, ps=ps, w_=w_, kc=kc, n=n, t0=t0: e.matmul(
                        ps[:, 0:n], lhsT=hT[:, kc, t0:t0 + 128], rhs=w_[:, kc, 0:n],
                        start=(kc == 0), stop=(kc == KC - 1)),
                        reads=[bw, b_hTs[t0 // pw]], writes=[bps], signal=(kc == KC - 1))
                k = nst % NST; nst += 1
                if ch["dt"] == F32:
                    st, bst = st32[k], b_s32[k]
                else:
                    st, bst = st16[k], b_s16[k]
                copy_op(P, P.evac_eng(), st[:, 0:n], ps[:, 0:n], [bps], [bst])
                P.dma("sync", ch["dst"][t0:t0 + 128, :], st[:, 0:n], reads=[bst])
    P.finish()


def fm_chunks(c0, c1, dst, dt, row0=0):
    out = []
    c = c0
    while c < c1:
        n = min(128, c1 - c)
        out.append(dict(kind="fm", ranges=[(c, c + n)], dst=dst[row0 + (c - c0):row0 + (c - c0) + n, :], dt=dt))
        c += n
    return out


def tm_chunks(c0, c1, dst, dt, col0=0):
    out = []
    c = c0
    while c < c1:
        n = min(512, c1 - c)
        out.append(dict(kind="tm", ranges=[(c, c + n)], dst=dst[:, col0 + (c - c0):col0 + (c - c0) + n], dt=dt))
        c += n
    return out


def phase_gla(nc, name, qT, kT, ktm, vtm, glrT, zT, wg2a, tri, gg, catT, T=T):
    P = Phase(nc, name)
    H, DK, DV, C = 4, 192, 384, 64
    G_ = min(512, T)
    NG = T // G_
    CPG = G_ // C
    ones = P.sb("ones", [128, 128], BF16); b_ones = P.buf()
    P.op("vector", lambda e: e.memset(ones[:], 1.0), writes=[b_ones])
    tri_sb = P.sb("tri", [64, 192], F32); b_tri = P.buf()
    P.dma("sync", tri_sb[:], tri[:, :], writes=[b_tri])
    triC = tri_sb[:, 0:64]; triR = tri_sb[:, 64:128]; maskA = tri_sb[:, 128:192]
    wg = P.sb("wg", [17, 768], F32); b_wg = P.buf()
    P.dma("sync", wg[:], wg2a[:, :], writes=[b_wg])
    gl = P.sb("gl", [17, T], F32); b_gl = P.buf()
    P.op("vector", lambda e: e.memset(gl[:], 1.0), writes=[b_gl])
    P.dma("sync", gl[0:16, :], glrT[:, :], writes=[b_gl])
    gg_sb = P.sb("gg", [128, 3], F32); b_gg = P.buf()
    P.dma("sync", gg_sb[:], gg[:, :], writes=[b_gg])
    S = P.sb("S", [128, 8, DV], F32); b_S = P.buf()
    Sb = P.sb("Sb", [128, 8, DV], BF16); b_Sb = P.buf()
    P.op("vector", lambda e: e.memset(S[:], 0.0), writes=[b_S])
    P.op("gpsimd", lambda e: e.memset(Sb[:], 0.0), writes=[b_Sb])
    NGB = 2
    qg = [P.sb(f"qg{i}", [128, 8, G_], BF16) for i in range(NGB)]; b_qg = P.bufs(NGB)
    kg = [P.sb(f"kg{i}", [128, 8, G_], BF16) for i in range(NGB)]; b_kg = P.bufs(NGB)
    zg = [P.sb(f"zg{i}", [128, 12, G_], F32) for i in range(NGB)]; b_zg = P.bufs(NGB)
    cg = [P.sb(f"cg{i}", [128, 12, G_], BF16) for i in range(NGB)]; b_cg = P.bufs(NGB)
    NCB = 3
    kt = [P.sb(f"kt{i}", [64, 768], BF16) for i in range(NCB)]; b_kt = P.bufs(NCB)
    vt = [P.sb(f"vt{i}", [64, 1536], BF16) for i in range(NCB)]; b_vt = P.bufs(NCB)
    e1 = P.sb("e1", [64, 768], F32); b_e1 = P.buf()
    sp = P.sb("sp", [64, 768], F32); b_sp = P.buf()
    er = P.sb("er", [64, 768], F32); b_er = P.buf()
    kpp2 = [P.sb(f"kpp{i}", [64, 768], BF16) for i in range(2)]; b_kpp2 = P.bufs(2)
    eq2 = [P.sb(f"eq{i}", [128, 8, C], F32) for i in range(2)]; b_eq2 = P.bufs(2)
    ek = P.sb("ek", [128, 8, C], F32); b_ek = P.buf()
    qp2 = [P.sb(f"qp{i}", [128, 8, C], BF16) for i in range(2)]; b_qp2 = P.bufs(2)
    kp = P.sb("kp", [128, 8, C], BF16); b_kp = P.buf()
    am2 = [P.sb(f"am{i}", [64, 4, C], BF16) for i in range(2)]; b_am2 = P.bufs(2)
    osq = P.sb("osq", [128, 12, C], BF16); b_osq = P.buf()
    rstd = P.sb("rstd", [128, 4, C], F32); b_rstd = P.buf()
    sz = P.sb("sz", [128, 12, C], F32); b_sz = P.buf()
    on = P.sb("on", [128, 12, C], F32); b_on = P.buf()
    pP = [P.ps(f"pP{i}", [128, 512], F32) for i in range(2)]; b_pP = P.bufs(2)
    pC = P.ps("pC", [128, 8, C], F32); b_pC = P.buf()
    pA = P.ps("pA", [128, 4, C], F32); b_pA = P.buf()
    pO = [P.ps(f"pO{i}", [128, 6, C], F32) for i in range(2)]; b_pO = P.bufs(2)
    pS = [P.ps(f"pS{i}", [128, 512], F32) for i in range(2)]; b_pS = P.bufs(2)
    qTa = qT.rearrange("(h k) t -> k h t", k=DK)
    kTa = kT.rearrange("(h k) t -> k h t", k=DK)
    zTv = zT.rearrange("(s p) t -> p s t", p=128)
    catv = catT[0:1536, :].rearrange("(s p) t -> p s t", p=128)
    qscale = float(DK) ** -0.5
    NCH = T // C

    def Ga(c):
        g, cc = divmod(c, CPG)
        gi = g % NGB
        g0 = g * G_
        t0 = g0 + cc * C
        tc = cc * C
        ci = c % NCB
        d2 = c % 2
        kpp, b_kpp = kpp2[d2], b_kpp2[d2]
        eq, b_eq = eq2[d2], b_eq2[d2]
        qp, b_qp = qp2[d2], b_qp2[d2]
        am, b_am = am2[d2], b_am2[d2]
        kt_ = kt[ci]
        if cc == 0:
            P.dma("sync", [qg[gi][:, 0:4, :], qg[gi][0:64, 4:8, :]],
                  [qTa[0:128, :, g0:g0 + G_], qTa[128:192, :, g0:g0 + G_]], writes=[b_qg[gi]])
            P.dma("sync", [kg[gi][:, 0:4, :], kg[gi][0:64, 4:8, :]],
                  [kTa[0:128, :, g0:g0 + G_], kTa[128:192, :, g0:g0 + G_]], writes=[b_kg[gi]])
            P.dma("sync", zg[gi][:], zTv[:, :, g0:g0 + G_], writes=[b_zg[gi]])
        P.dma("sync", kt[ci][:], ktm[t0:t0 + C, :], writes=[b_kt[ci]])
        P.dma("sync", vt[ci][:], vtm[t0:t0 + C, :], writes=[b_vt[ci]])
        for hf in range(2):
            P.op("tensor", lambda e, hf=hf, t0=t0: e.matmul(pP[hf][0:64, 0:384], lhsT=gl[0:17, t0:t0 + C], rhs=wg[0:17, hf * 384:(hf + 1) * 384], start=True, stop=True),
                 reads=[b_gl, b_wg], writes=[b_pP[hf]])
        for hf in range(2):
            P.op("scalar", lambda e, hf=hf: e.activation(out=e1[:, hf * 384:(hf + 1) * 384], in_=pP[hf][0:64, 0:384], func=AF.Exp, scale=-1.0),
                 reads=[b_pP[hf]], writes=[b_e1])
        P.op("scalar", lambda e: e.activation(out=sp[:], in_=e1[:], func=AF.Ln, bias=1.0), reads=[b_e1], writes=[b_sp])

    def Gb(c):
        g, cc = divmod(c, CPG)
        gi = g % NGB
        g0 = g * G_
        t0 = g0 + cc * C
        tc = cc * C
        ci = c % NCB
        d2 = c % 2
        kpp, b_kpp = kpp2[d2], b_kpp2[d2]
        eq, b_eq = eq2[d2], b_eq2[d2]
        qp, b_qp = qp2[d2], b_qp2[d2]
        am, b_am = am2[d2], b_am2[d2]
        kt_ = kt[ci]
        for hf in range(2):
            P.op("tensor", lambda e, hf=hf: e.matmul(pP[hf][0:64, 0:384], lhsT=triR, rhs=sp[:, hf * 384:(hf + 1) * 384], start=True, stop=True),
                 reads=[b_tri, b_sp], writes=[b_pP[hf]])
        for hf in range(2):
            P.op("scalar", lambda e, hf=hf: e.activation(out=er[:, hf * 384:(hf + 1) * 384], in_=pP[hf][0:64, 0:384], func=AF.Exp),
                 reads=[b_pP[hf]], writes=[b_er])
        P.op("vector", lambda e, kt_=kt_: e.tensor_tensor(out=kpp[:], in0=kt_[:], in1=er[:], op=ALU.mult), reads=[b_kt[ci], b_er], writes=[b_kpp])
        for h in range(H):
            P.op("tensor", lambda e, h=h: e.matmul(pC[:, h, :], lhsT=sp[:, h * DK:h * DK + 128], rhs=triC, start=True, stop=True),
                 reads=[b_sp, b_tri], writes=[b_pC], signal=False)
        for h in range(H):
            P.op("tensor", lambda e, h=h: e.matmul(pC[0:64, 4 + h, :], lhsT=sp[:, h * DK + 128:(h + 1) * DK], rhs=triC, start=True, stop=True),
                 reads=[b_sp, b_tri], writes=[b_pC], signal=(h == H - 1))
        P.op("scalar", lambda e: e.activation(out=eq[:, 0:4, :], in_=pC[:, 0:4, :], func=AF.Exp), reads=[b_pC], writes=[b_eq])
        P.op("scalar", lambda e: e.activation(out=eq[0:64, 4:8, :], in_=pC[0:64, 4:8, :], func=AF.Exp), reads=[b_pC], writes=[b_eq])
        P.op("scalar", lambda e: e.activation(out=ek[:, 0:4, :], in_=pC[:, 0:4, :], func=AF.Exp, scale=-1.0), reads=[b_pC], writes=[b_ek])
        P.op("scalar", lambda e: e.activation(out=ek[0:64, 4:8, :], in_=pC[0:64, 4:8, :], func=AF.Exp, scale=-1.0), reads=[b_pC], writes=[b_ek])
        qg_, kg_ = qg[gi], kg[gi]
        P.op("vector", lambda e, qg_=qg_, tc=tc: e.scalar_tensor_tensor(out=qp[:, 0:4, :], in0=qg_[:, 0:4, tc:tc + C], scalar=qscale, in1=eq[:, 0:4, :], op0=ALU.mult, op1=ALU.mult),
             reads=[b_qg[gi], b_eq], writes=[b_qp])
        P.op("vector", lambda e, qg_=qg_, tc=tc: e.scalar_tensor_tensor(out=qp[0:64, 4:8, :], in0=qg_[0:64, 4:8, tc:tc + C], scalar=qscale, in1=eq[0:64, 4:8, :], op0=ALU.mult, op1=ALU.mult),
             reads=[b_qg[gi], b_eq], writes=[b_qp])
        P.op("gpsimd", lambda e, kg_=kg_, tc=tc: e.tensor_tensor(out=kp[:, 0:4, :], in0=kg_[:, 0:4, tc:tc + C], in1=ek[:, 0:4, :], op=ALU.mult),
             reads=[b_kg[gi], b_ek], writes=[b_kp])
        P.op("gpsimd", lambda e, kg_=kg_, tc=tc: e.tensor_tensor(out=kp[0:64, 4:8, :], in0=kg_[0:64, 4:8, tc:tc + C], in1=ek[0:64, 4:8, :], op=ALU.mult),
             reads=[b_kg[gi], b_ek], writes=[b_kp])

    def Gc(c):
        g, cc = divmod(c, CPG)
        gi = g % NGB
        g0 = g * G_
        t0 = g0 + cc * C
        tc = cc * C
        ci = c % NCB
        d2 = c % 2
        kpp, b_kpp = kpp2[d2], b_kpp2[d2]
        eq, b_eq = eq2[d2], b_eq2[d2]
        qp, b_qp = qp2[d2], b_qp2[d2]
        am, b_am = am2[d2], b_am2[d2]
        kt_ = kt[ci]
        for h in range(H):
            P.op("tensor", lambda e, h=h: e.matmul(pA[0:64, h, :], lhsT=kp[:, h, :], rhs=qp[:, h, :], start=True, stop=False),
                 reads=[b_kp, b_qp], writes=[b_pA], signal=False)
            P.op("tensor", lambda e, h=h: e.matmul(pA[0:64, h, :], lhsT=kp[0:64, 4 + h, :], rhs=qp[0:64, 4 + h, :], start=False, stop=True),
                 reads=[b_kp, b_qp], writes=[b_pA], signal=(h == H - 1))
        for h in range(H):
            P.op("vector", lambda e, h=h: e.tensor_tensor(out=am[:, h, :], in0=pA[0:64, h, :], in1=maskA, op=ALU.mult),
                 reads=[b_pA, b_tri], writes=[b_am])


    def Ua(c):
        g, cc = divmod(c, CPG)
        gi = g % NGB
        g0 = g * G_
        tc = cc * C
        ci = c % NCB
        d2 = c % 2
        kpp, b_kpp = kpp2[d2], b_kpp2[d2]
        eq, b_eq = eq2[d2], b_eq2[d2]
        qp, b_qp = qp2[d2], b_qp2[d2]
        am, b_am = am2[d2], b_am2[d2]
        vt_ = vt[ci]
        for h in range(H):
            po = pO[h // 2]; bpo = b_pO[h // 2]
            for j in range(3):
                sl = (h % 2) * 3 + j
                P.op("tensor", lambda e, h=h, j=j, po=po, sl=sl: e.matmul(po[:, sl, :], lhsT=Sb[:, h, j * 128:(j + 1) * 128], rhs=qp[:, h, :], start=True, stop=False),
                     reads=[b_Sb, b_qp], writes=[bpo], signal=False)
                P.op("tensor", lambda e, h=h, j=j, po=po, sl=sl: e.matmul(po[:, sl, :], lhsT=Sb[0:64, 4 + h, j * 128:(j + 1) * 128], rhs=qp[0:64, 4 + h, :], start=False, stop=False),
                     reads=[b_Sb, b_qp], writes=[bpo], signal=False)
                P.op("tensor", lambda e, h=h, j=j, po=po, sl=sl, vt_=vt_: e.matmul(po[:, sl, :], lhsT=vt_[:, h * DV + j * 128:h * DV + (j + 1) * 128], rhs=am[:, h, :], start=False, stop=True),
                     reads=[b_vt[ci], b_am], writes=[bpo], signal=(h % 2 == 1 and j == 2))

    def Ub(c):
        g, cc = divmod(c, CPG)
        gi = g % NGB
        g0 = g * G_
        tc = cc * C
        ci = c % NCB
        d2 = c % 2
        kpp, b_kpp = kpp2[d2], b_kpp2[d2]
        eq, b_eq = eq2[d2], b_eq2[d2]
        qp, b_qp = qp2[d2], b_qp2[d2]
        am, b_am = am2[d2], b_am2[d2]
        vt_ = vt[ci]
        for sl in range(8):
            h = sl % 4
            rows = 128 if sl < 4 else 64
            c0 = h * DK + (0 if sl < 4 else 128)
            ps_ = pS[sl % 2]; bps_ = b_pS[sl % 2]
            P.op("tensor", lambda e, ps_=ps_, rows=rows, c0=c0, h=h, vt_=vt_: e.matmul(ps_[0:rows, 0:DV], lhsT=kpp[:, c0:c0 + rows], rhs=vt_[:, h * DV:(h + 1) * DV], start=True, stop=True),
                 reads=[b_kpp, b_vt[ci]], writes=[bps_])
            P.op("vector", lambda e, ps_=ps_, rows=rows, sl=sl: e.scalar_tensor_tensor(out=S[0:rows, sl, :], in0=S[0:rows, sl, :], scalar=eq[0:rows, sl, C - 1:C], in1=ps_[0:rows, 0:DV], op0=ALU.mult, op1=ALU.add),
                 reads=[bps_, b_eq], writes=[b_S])
        P.op("scalar", lambda e: e.activation(out=Sb[:, 0:4, :], in_=S[:, 0:4, :], func=AF.Copy), reads=[b_S], writes=[b_Sb])
        P.op("gpsimd", lambda e: e.tensor_copy(out=Sb[0:64, 4:8, :], in_=S[0:64, 4:8, :]), reads=[b_S], writes=[b_Sb])

    def Uc(c):
        g, cc = divmod(c, CPG)
        gi = g % NGB
        g0 = g * G_
        tc = cc * C
        ci = c % NCB
        d2 = c % 2
        kpp, b_kpp = kpp2[d2], b_kpp2[d2]
        eq, b_eq = eq2[d2], b_eq2[d2]
        qp, b_qp = qp2[d2], b_qp2[d2]
        am, b_am = am2[d2], b_am2[d2]
        vt_ = vt[ci]
        for hp in range(2):
            P.op("scalar", lambda e, hp=hp: e.activation(out=osq[:, hp * 6:(hp + 1) * 6, :], in_=pO[hp][:, :, :], func=AF.Square),
                 reads=[b_pO[hp]], writes=[b_osq])
        for h in range(H):
            for j in range(3):
                P.op("tensor", lambda e, h=h, j=j: e.matmul(pC[:, h, :], lhsT=ones[:, :], rhs=osq[:, h * 3 + j, :], start=(j == 0), stop=(j == 2)),
                     reads=[b_ones, b_osq], writes=[b_pC], signal=(h == H - 1 and j == 2))
        P.op("vector", lambda e: e.tensor_scalar(out=rstd[:], in0=pC[:, 0:4, :], scalar1=1.0 / DV, scalar2=EPS, op0=ALU.mult, op1=ALU.add),
             reads=[b_pC], writes=[b_rstd])
        P.op("scalar", lambda e: e.activation(out=rstd[:], in_=rstd[:], func=AF.Sqrt), writes=[b_rstd])
        P.op("vector", lambda e: e.reciprocal(out=rstd[:], in_=rstd[:]), writes=[b_rstd])
        zg_ = zg[gi]
        P.op("scalar", lambda e, zg_=zg_, tc=tc: e.activation(out=sz[:], in_=zg_[:, :, tc:tc + C], func=AF.Silu), reads=[b_zg[gi]], writes=[b_sz])
        for hp in range(2):
            for j in range(3):
                P.op("vector", lambda e, hp=hp, j=j: e.scalar_tensor_tensor(
                    out=on[:, hp * 6 + j:hp * 6 + 6:3, :], in0=pO[hp][:, j:6:3, :], scalar=gg_sb[:, j:j + 1],
                    in1=rstd[:, hp * 2:hp * 2 + 2, :], op0=ALU.mult, op1=ALU.mult),
                    reads=[b_pO[hp], b_gg, b_rstd], writes=[b_on])
        cg_ = cg[gi]
        P.op("gpsimd", lambda e, cg_=cg_, tc=tc: e.tensor_tensor(out=cg_[:, :, tc:tc + C], in0=on[:], in1=sz[:], op=ALU.mult),
             reads=[b_on, b_sz], writes=[b_cg[gi]])
        if cc == CPG - 1:
            P.dma("sync", catv[:, :, g0:g0 + G_], cg[gi][:], reads=[b_cg[gi]])


    Ga(0); Gb(0); Gc(0)
    for c in range(NCH):
        nxt = c + 1 < NCH
        if nxt:
            Ga(c + 1)
        Ua(c)
        if nxt:
            Gb(c + 1)
        Ub(c)
        if nxt:
            Gc(c + 1)
        Uc(c)
    P.finish()

def phase_attn(nc, name, KTn, KTr, V, QTn, QTr, zT, masks, catT, row0, H, Tk, Tq, scale, nkb):
    P = Phase(nc, name)
    QG = min(512, Tq)
    NQG = Tq // QG
    NKB = Tk // 128
    HG = 4
    ones = P.sb("ones", [128, 128], BF16); b_ones = P.buf()
    P.op("vector", lambda e: e.memset(ones[:], 1.0), writes=[b_ones])
    if KTr is not None:
        kr = P.sb("kr", [64, Tk], BF16); b_kr = P.buf()
        P.dma("sync", kr[:], KTr[:, :], writes=[b_kr])
    if masks is not None:
        mk = P.sb("mk", [128, 8, 512], F32); b_mk = P.buf()
        P.dma("sync", mk[:], masks.rearrange("j p q -> p j q"), writes=[b_mk])
    vg = [P.sb(f"vg{i}", [128, NKB, HG * 128], BF16) for i in range(2)]; b_vg = P.bufs(2)
    kh = [P.sb(f"kh{i}", [128, Tk], BF16) for i in range(2)]; b_kh = P.bufs(2)
    NQ = 4
    qn = [P.sb(f"qn{i}", [128, QG], BF16) for i in range(NQ)]; b_qn = P.bufs(NQ)
    qr = [P.sb(f"qr{i}", [64, QG], BF16) for i in range(NQ)]; b_qr = P.bufs(NQ)
    zt = [P.sb(f"zt{i}", [128, QG], F32) for i in range(NQ)]; b_zt = P.bufs(NQ)
    NP = 4
    pt = [P.sb(f"pt{i}", [128, QG], BF16) for i in range(NP)]; b_pt = P.bufs(NP)
    pf = [P.sb(f"pf{i}", [128, QG], F32) for i in range(2)]; b_pf = P.bufs(2)
    NE = 2
    rden = [P.sb(f"rden{i}", [128, QG], F32) for i in range(NE)]; b_rden = P.bufs(NE)
    sz = [P.sb(f"sz{i}", [128, QG], F32) for i in range(NE)]; b_sz = P.bufs(NE)
    t1 = [P.sb(f"t1{i}", [128, QG], F32) for i in range(NE)]; b_t1 = P.bufs(NE)
    co = [P.sb(f"co{i}", [128, QG], BF16) for i in range(4)]; b_co = P.bufs(4)
    NS = 4
    pS = [P.ps(f"pS{i}", [128, 512], F32) for i in range(NS)]; b_pS = P.bufs(NS)
    pO = [P.ps(f"pO{i}", [128, 512], F32) for i in range(2)]; b_pO = P.bufs(2)
    pD = [P.ps(f"pD{i}", [128, 512], F32) for i in range(2)]; b_pD = P.bufs(2)
    Vv = V.rearrange("(kb p) n -> p kb n", p=128)
    pairs = [(h, g) for h in range(H) for g in range(NQG)]
    items = []
    for pi, (h, g) in enumerate(pairs):
        nk = nkb(g)
        for j in range(nk):
            items.append((pi, h, g, j, nk))

    def load_head(h):
        if h >= H:
            return
        hg, hh = divmod(h, HG)
        if hh == 0:
            P.dma("sync", vg[hg % 2][:], Vv[:, :, hg * HG * 128:(hg + 1) * HG * 128], writes=[b_vg[hg % 2]])
        P.dma("sync", kh[h % 2][:], KTn[h * 128:(h + 1) * 128, :], writes=[b_kh[h % 2]])

    def load_pair(pi):
        if pi >= len(pairs):
            return
        h, g = pairs[pi]
        qi = pi % NQ
        q0 = g * QG
        P.dma("sync", qn[qi][:], QTn[h * 128:(h + 1) * 128, q0:q0 + QG], writes=[b_qn[qi]])
        if QTr is not None:
            P.dma("sync", qr[qi][:], QTr[h * 64:(h + 1) * 64, q0:q0 + QG], writes=[b_qr[qi]])
        P.dma("sync", zt[qi][:], zT[h * 128:(h + 1) * 128, q0:q0 + QG], writes=[b_zt[qi]])

    slots = {}
    cnt = dict(nit=0, npf=0)

    def score(idx):
        pi, h, g, j, nk = items[idx]
        if j == 0:
            if g == 0:
                load_head(h + 1)
            load_pair(pi + 2)
        qi = pi % NQ
        k_ = kh[h % 2]; bk = b_kh[h % 2]
        nit = cnt["nit"]; cnt["nit"] += 1
        ps = pS[nit % NS]; bps = b_pS[nit % NS]
        p_ = pt[nit % NP]; bp = b_pt[nit % NP]
        slots[idx] = (p_, bp)
        P.op("tensor", lambda e, ps=ps, k_=k_, j=j, qi=qi: e.matmul(ps[:, 0:QG], lhsT=k_[:, j * 128:(j + 1) * 128], rhs=qn[qi][:], start=True, stop=(QTr is None)),
             reads=[bk, b_qn[qi]], writes=[bps], signal=(QTr is None))
        if QTr is not None:
            P.op("tensor", lambda e, ps=ps, j=j, qi=qi: e.matmul(ps[:, 0:QG], lhsT=kr[:, j * 128:(j + 1) * 128], rhs=qr[qi][:], start=False, stop=True),
                 reads=[b_kr, b_qr[qi]], writes=[bps])
        jj = j - (nk - 8)
        if masks is not None and jj >= 0:
            npf = cnt["npf"]; cnt["npf"] += 1
            f_ = pf[npf % 2]; bf_ = b_pf[npf % 2]
            P.op("scalar", lambda e, ps=ps, f_=f_: e.activation(out=f_[:], in_=ps[:, 0:QG], func=AF.Exp, scale=scale), reads=[bps], writes=[bf_])
            eng = "vector" if npf % 2 else "gpsimd"
            P.op(eng, lambda e, f_=f_, p_=p_, jj=jj: e.tensor_tensor(out=p_[:], in0=f_[:], in1=mk[:, jj, :], op=ALU.mult), reads=[bf_, b_mk], writes=[bp])
        else:
            P.op("scalar", lambda e, ps=ps, p_=p_: e.activation(out=p_[:], in_=ps[:, 0:QG], func=AF.Exp, scale=scale), reads=[bps], writes=[bp])

    def accum(idx):
        pi, h, g, j, nk = items[idx]
        hg, hh = divmod(h, HG)
        v_ = vg[hg % 2]; bv = b_vg[hg % 2]
        po = pO[pi % 2]; bpo = b_pO[pi % 2]
        pd = pD[pi % 2]; bpd = b_pD[pi % 2]
        p_, bp = slots.pop(idx)
        P.op("tensor", lambda e, po=po, v_=v_, j=j, hh=hh, p_=p_, nk=nk: e.matmul(po[:, 0:QG], lhsT=v_[:, j, hh * 128:(hh + 1) * 128], rhs=p_[:], start=(j == 0), stop=(j == nk - 1)),
             reads=[bv, bp], writes=[bpo], signal=False)
        P.op("tensor", lambda e, pd=pd, p_=p_, j=j, nk=nk: e.matmul(pd[:, 0:QG], lhsT=ones[:, :], rhs=p_[:], start=(j == 0), stop=(j == nk - 1)),
             reads=[b_ones, bp], writes=[bpd], signal=True)
        if j == nk - 1:
            qi = pi % NQ
            ei = pi % NE
            ci = pi % 4
            q0 = g * QG
            P.op("vector", lambda e, pd=pd, ei=ei: e.reciprocal(out=rden[ei][:], in_=pd[:, 0:QG]), reads=[bpd], writes=[b_rden[ei]])
            P.op("scalar", lambda e, qi=qi, ei=ei: e.activation(out=sz[ei][:], in_=zt[qi][:], func=AF.Silu), reads=[b_zt[qi]], writes=[b_sz[ei]])
            P.op("vector", lambda e, po=po, ei=ei: e.tensor_tensor(out=t1[ei][:], in0=po[:, 0:QG], in1=rden[ei][:], op=ALU.mult), reads=[bpo, b_rden[ei]], writes=[b_t1[ei]])
            P.op("gpsimd", lambda e, ci=ci, ei=ei: e.tensor_tensor(out=co[ci][:], in0=t1[ei][:], in1=sz[ei][:], op=ALU.mult), reads=[b_t1[ei], b_sz[ei]], writes=[b_co[ci]])
            P.dma("sync", catT[row0 + h * 128:row0 + (h + 1) * 128, q0:q0 + QG], co[ci][:], reads=[b_co[ci]])

    load_head(0)
    load_pair(0)
    load_pair(1)
    LA = 2
    n = len(items)
    for idx in range(min(LA, n)):
        score(idx)
    for idx in range(n):
        if idx + LA < n:
            score(idx + LA)
        accum(idx)
    P.finish()


def phase_outproj(nc, name, catT, W, gain_d, resT, outT, Tn):
    P = Phase(nc, name)
    KC = 16
    TT = 512
    ones = P.sb("ones", [128, 128], BF16); b_ones = P.buf()
    P.op("vector", lambda e: e.memset(ones[:], 1.0), writes=[b_ones])
    g = P.sb("g", [128, KC], F32); b_g = P.buf()
    P.dma("sync", g[:], gain_d[:, :], writes=[b_g])
    w = P.sb("w", [128, KC, D], BF16); b_w = [P.buf() for _ in range(4)]
    Wv = W.rearrange("(kc p) n -> p kc n", p=128)
    for i in range(4):
        P.dma("gpsimd", w[:, :, i * 512:(i + 1) * 512], Wv[:, :, i * 512:(i + 1) * 512], writes=[b_w[i]])
    ct = [P.sb(f"c{i}", [128, KC, TT], BF16) for i in range(2)]; b_ct = P.bufs(2)
    xr = P.sb("xr", [128, KC, TT], F32); b_xr = P.bufs(4)
    y = P.sb("y", [128, KC, TT], F32); b_y = P.bufs(KC)
    sq = [P.sb(f"sq{i}", [128, TT], BF16) for i in range(3)]; b_sq = P.bufs(3)
    rss = [P.sb(f"rs{i}", [128, TT], F32) for i in range(2)]; b_rss = P.bufs(2)
    NPS = 5
    pss = [P.ps(f"ps{i}", [128, 512], F32) for i in range(NPS)]; b_ps = P.bufs(NPS)
    pn = [P.ps(f"pn{i}", [128, 512], F32) for i in range(2)]; b_pn = P.bufs(2)
    catv = catT.rearrange("(kc p) t -> p kc t", p=128)
    resv = resT.rearrange("(kc p) t -> p kc t", p=128)
    outv = outT.rearrange("(kc p) t -> p kc t", p=128)
    nps = 0
    nsq = 0
    for it in range(Tn // TT):
        t0 = it * TT
        i = it % 2
        rs = rss[i]; b_rs = b_rss[i]
        if it == 0:
            P.dma("sync", ct[0][:], catv[:, :, 0:TT], writes=[b_ct[0]])
        if it + 1 < Tn // TT:
            P.dma("sync", ct[(it + 1) % 2][:], catv[:, :, t0 + TT:t0 + 2 * TT], writes=[b_ct[(it + 1) % 2]])
        for q in range(4):
            P.dma("sync", xr[:, q * 4:(q + 1) * 4, :], resv[:, q * 4:(q + 1) * 4, t0:t0 + TT], writes=[b_xr[q]])
        pn_ = pn[it % 2]; bpn = b_pn[it % 2]
        pend = []

        def stat_mm(j, n, pn_=pn_, bpn=bpn):
            P.op("tensor", lambda e, pn_=pn_, j=j, n=n: e.matmul(pn_[:, 0:TT], lhsT=ones[:, :], rhs=sq[j][:], start=(n == 0), stop=(n == KC - 1)),
                 reads=[b_ones, b_sq[j]], writes=[bpn], signal=True)

        for n in range(KC):
            ps = pss[nps % NPS]; bps = b_ps[nps % NPS]; nps += 1
            for kc in range(KC):
                P.op("tensor", lambda e, ps=ps, kc=kc, n=n, i=i: e.matmul(ps[:, 0:TT], lhsT=w[:, kc, n * 128:(n + 1) * 128], rhs=ct[i][:, kc, :], start=(kc == 0), stop=(kc == KC - 1)),
                     reads=[b_w[n // 4], b_ct[i]], writes=[bps], signal=(kc == KC - 1))
            P.op("scalar", lambda e, ps=ps, n=n: e.activation(out=y[:, n, :], in_=ps[:, 0:TT], func=AF.Copy), reads=[bps], writes=[b_y[n]])
            j = nsq % 3; nsq += 1
            P.op("scalar", lambda e, n=n, j=j: e.activation(out=sq[j][:], in_=y[:, n, :], func=AF.Square), reads=[b_y[n]], writes=[b_sq[j]])
            pend.append((j, n))
            if len(pend) > 1:
                stat_mm(*pend.pop(0))
        while pend:
            stat_mm(*pend.pop(0))
        P.op("vector", lambda e, pn_=pn_, rs=rs: e.tensor_scalar(out=rs[:], in0=pn_[:, 0:TT], scalar1=1.0 / D, scalar2=EPS, op0=ALU.mult, op1=ALU.add), reads=[bpn], writes=[b_rs])
        P.op("scalar", lambda e, rs=rs: e.activation(out=rs[:], in_=rs[:], func=AF.Sqrt), writes=[b_rs])
        P.op("vector", lambda e, rs=rs: e.reciprocal(out=rs[:], in_=rs[:]), writes=[b_rs])
        for n in range(KC):
            P.op("vector", lambda e, n=n, rs=rs: e.scalar_tensor_tensor(out=y[:, n, :], in0=y[:, n, :], scalar=g[:, n:n + 1], in1=rs[:], op0=ALU.mult, op1=ALU.mult),
                 reads=[b_g, b_rs], writes=[b_y[n]])
            P.op("vector", lambda e, n=n: e.tensor_tensor(out=xr[:, n, :], in0=xr[:, n, :], in1=y[:, n, :], op=ALU.add),
                 reads=[b_y[n]], writes=[b_xr[n // 4]])
            if n % 4 == 3:
                q = n // 4
                P.dma("sync", outv[:, q * 4:(q + 1) * 4, t0:t0 + TT], xr[:, q * 4:(q + 1) * 4, :], reads=[b_xr[q]])
    P.finish()


def phase_rope(nc, name, raw, pos, cst, out, NH, Tn):
    P = Phase(nc, name)
    TW = 1024
    c = P.sb("c", [64, 2], F32); b_c = P.buf()
    P.dma("sync", c[:], cst[:, :], writes=[b_c])
    hp = P.sb("hp", [64, 1], F32); b_hp = P.buf()
    P.op("vector", lambda e: e.memset(hp[:], float(np.pi / 2)), writes=[b_hp])
    cs = P.sb("cs", [64, Tn], F32); b_cs = P.buf()
    sn = P.sb("sn", [64, Tn], F32); b_sn = P.buf()
    pi_ = P.sb("pi", [64, TW], I32); b_pi = P.buf()
    a0 = P.sb("a0", [64, TW], F32); b_a0 = P.buf()
    a1 = P.sb("a1", [64, TW], F32); b_a1 = P.buf()
    ki = P.sb("ki", [64, TW], I32); b_ki = P.buf()
    kf = P.sb("kf", [64, TW], F32); b_kf = P.buf()
    f1 = P.sb("f1", [64, TW], F32); b_f1 = P.buf()
    f2 = P.sb("f2", [64, TW], F32); b_f2 = P.buf()
    C1 = 6.28125
    C2 = float(2 * np.pi - 6.28125)
    PI = float(np.pi)
    for it in range(Tn // TW):
        t0 = it * TW
        P.dma("sync", pi_[:], pos[:, t0:t0 + TW].partition_broadcast(64), writes=[b_pi])
        P.op("vector", lambda e: e.tensor_copy(out=a0[:], in_=pi_[:]), reads=[b_pi], writes=[b_a0])
        P.op("vector", lambda e: e.tensor_scalar(out=a0[:], in0=a0[:], scalar1=c[:, 0:1], scalar2=None, op0=ALU.mult), reads=[b_c], writes=[b_a0])
        P.op("vector", lambda e: e.tensor_scalar(out=a1[:], in0=a0[:], scalar1=float(1 / (2 * np.pi)), scalar2=None, op0=ALU.mult), reads=[b_a0], writes=[b_a1])
        P.op("vector", lambda e: e.tensor_copy(out=ki[:], in_=a1[:]), reads=[b_a1], writes=[b_ki])
        P.op("vector", lambda e: e.tensor_copy(out=kf[:], in_=ki[:]), reads=[b_ki], writes=[b_kf])
        P.op("vector", lambda e: e.scalar_tensor_tensor(out=a1[:], in0=kf[:], scalar=-C1, in1=a0[:], op0=ALU.mult, op1=ALU.add), reads=[b_kf, b_a0], writes=[b_a1])
        P.op("vector", lambda e: e.scalar_tensor_tensor(out=a0[:], in0=kf[:], scalar=-C2, in1=a1[:], op0=ALU.mult, op1=ALU.add), reads=[b_kf, b_a1], writes=[b_a0])
        P.op("vector", lambda e: e.tensor_scalar(out=f1[:], in0=a0[:], scalar1=PI, scalar2=-2 * PI, op0=ALU.is_gt, op1=ALU.mult), reads=[b_a0], writes=[b_f1])
        P.op("vector", lambda e: e.tensor_scalar(out=f2[:], in0=a0[:], scalar1=-PI, scalar2=2 * PI, op0=ALU.is_lt, op1=ALU.mult), reads=[b_a0], writes=[b_f2])
        P.op("vector", lambda e: e.tensor_tensor(out=f1[:], in0=f1[:], in1=f2[:], op=ALU.add), reads=[b_f2], writes=[b_f1])
        P.op("vector", lambda e: e.tensor_tensor(out=a1[:], in0=a0[:], in1=f1[:], op=ALU.add), reads=[b_a0, b_f1], writes=[b_a1])
        P.op("scalar", lambda e, t0=t0: e.activation(out=sn[:, t0:t0 + TW], in_=a1[:], func=AF.Sin), reads=[b_a1], writes=[b_sn])
        P.op("scalar", lambda e: e.activation(out=f2[:], in_=a1[:], func=AF.Abs), reads=[b_a1], writes=[b_f2])
        P.op("scalar", lambda e, t0=t0: e.activation(out=cs[:, t0:t0 + TW], in_=f2[:], func=AF.Sin, scale=-1.0, bias=hp[:]), reads=[b_f2, b_hp], writes=[b_cs])
        P.op("vector", lambda e, t0=t0: e.tensor_scalar(out=sn[:, t0:t0 + TW], in0=sn[:, t0:t0 + TW], scalar1=c[:, 1:2], scalar2=None, op0=ALU.mult), reads=[b_c], writes=[b_sn])
    xa = [P.sb(f"xa{i}", [64, Tn], F32) for i in range(2)]; b_xa = P.bufs(2)
    xb = [P.sb(f"xb{i}", [64, Tn], F32) for i in range(2)]; b_xb = P.bufs(2)
    ob = [P.sb(f"ob{i}", [64, Tn], BF16) for i in range(2)]; b_ob = P.bufs(2)
    def ld(h):
        i = h % 2
        P.dma("sync", xa[i][:], raw[h * 128:h * 128 + 64, :], writes=[b_xa[i]])
        P.dma("sync", xb[i][:], raw[h * 128 + 64:h * 128 + 128, :], writes=[b_xb[i]])

    ld(0)
    for h in range(NH):
        i = h % 2
        if h + 1 < NH:
            ld(h + 1)
        P.op("vector", lambda e, i=i: e.tensor_tensor(out=xa[i][:], in0=xa[i][:], in1=cs[:], op=ALU.mult), reads=[b_cs], writes=[b_xa[i]])
        P.op("gpsimd", lambda e, i=i: e.tensor_tensor(out=xb[i][:], in0=xb[i][:], in1=sn[:], op=ALU.mult), reads=[b_sn], writes=[b_xb[i]])
        P.op("vector", lambda e, i=i: e.tensor_tensor(out=ob[i][:], in0=xa[i][:], in1=xb[i][:], op=ALU.add), reads=[b_xa[i], b_xb[i]], writes=[b_ob[i]])
        P.dma("sync", out[h * 64:(h + 1) * 64, :], ob[i][:], reads=[b_ob[i]])
    P.finish()


GAINS = ["a_pre", "a_mem", "a_post", "kv_in", "b_pre", "b_mem", "b_post"]


def build(stop_after=None, debug=()):
    nc = bass.Bass("TRN2", target_bir_lowering=False)
    t = {}

    def inp(name, shape, dt=F32):
        t[name] = nc.dram_tensor(name, list(shape), dt, kind="ExternalInput").ap()
        return t[name]

    def scr(name, shape, dt):
        kind = "ExternalOutput" if name in debug else "Internal"
        t[name] = nc.dram_tensor(name, list(shape), dt, kind=kind).ap()
        return t[name]

    inp("xT", [D, T]); inp("memT", [D, NM])
    inp("pos", [1, T], I32); inp("pos_own", [1, TO], I32)
    inp("masks", [8, 128, 512]); inp("sel", [128, 2]); inp("cst", [64, 2]); inp("tri", [64, 192])
    inp("a_w_in", [D, 5648]); inp("wg2a", [17, 768]); inp("a_w_mem_kv", [D, 1024]); inp("a_w_out", [D, D])
    inp("w_dkv", [D, 576]); inp("w_uk", [512, 1536]); inp("w_uv", [512, 1536])
    inp("b_w_in", [D, 3072]); inp("b_w_uq", [512, 2304]); inp("b_w_mem_kv", [D, 1024]); inp("b_w_out", [D, D])
    for gname in GAINS:
        inp("g_" + gname, [128, 16])
    inp("g_kv", [128, 4]); inp("g_bq", [128, 4]); inp("g_gla", [128, 3])
    outT = nc.dram_tensor("outT", [D, TO], F32, kind="ExternalOutput").ap()

    phases = []

    def ph(name, fn):
        phases.append((name, fn))

    for L in ("a", "b"):
        scr(f"memh_{L}", [D, NM], BF16); scr(f"memKT_{L}", [512, NM], BF16); scr(f"memV_{L}", [NM, 512], BF16)
        ph(f"mn{L}", lambda L=L: phase_norm(nc, f"mn{L}", t["memT"], t[f"g_{L}_mem"], t[f"memh_{L}"], D, NM))
        ph(f"mp{L}", lambda L=L: phase_proj(nc, f"mp{L}", t[f"memh_{L}"], t[f"{L}_w_mem_kv"], D, NM,
                                             fm_chunks(0, 512, t[f"memKT_{L}"], BF16) + tm_chunks(512, 1024, t[f"memV_{L}"], BF16)))
    scr("hT", [D, T], BF16)
    ph("an", lambda: phase_norm(nc, "an", t["xT"], t["g_a_pre"], t["hT"], D, T))
    scr("qT", [768, T], BF16); scr("kT", [768, T], BF16); scr("glrT", [16, T], F32); scr("zT", [1536, T], F32)
    scr("mqT", [512, T], BF16); scr("mzT", [512, T], F32); scr("ktm", [T, 768], BF16); scr("vtm", [T, 1536], BF16)
    ph("ap", lambda: phase_proj(nc, "ap", t["hT"], t["a_w_in"], D, T,
                                fm_chunks(0, 768, t["qT"], BF16) + fm_chunks(768, 1536, t["kT"], BF16)
                                + fm_chunks(3072, 3088, t["glrT"], F32) + fm_chunks(3088, 4624, t["zT"], F32)
                                + fm_chunks(4624, 5136, t["mqT"], BF16) + fm_chunks(5136, 5648, t["mzT"], F32)
                                + tm_chunks(768, 1536, t["ktm"], BF16) + tm_chunks(1536, 3072, t["vtm"], BF16)))
    scr("catT", [D, T], BF16)
    ph("ag", lambda: phase_gla(nc, "ag", t["qT"], t["kT"], t["ktm"], t["vtm"], t["glrT"], t["zT"], t["wg2a"], t["tri"], t["g_gla"], t["catT"]))
    ph("am", lambda: phase_attn(nc, "am", t["memKT_a"], None, t["memV_a"], t["mqT"], None, t["mzT"], None, t["catT"], 1536,
                                4, NM, T, 128.0 ** -0.5, lambda g: 2))
    scr("x1T", [D, T], F32)
    ph("ao", lambda: phase_outproj(nc, "ao", t["catT"], t["a_w_out"], t["g_a_post"], t["xT"], t["x1T"], T))
    scr("cT", [512, T], F32); scr("krawT", [128, T], F32)
    ph("kp", lambda: phase_normproj(nc, "kp", t["x1T"], t["g_kv_in"], D, T,
                                    [(t["w_dkv"], fm_chunks(0, 512, t["cT"], F32)
                                      + [dict(kind="fm", ranges=[(512, 576), (544, 576), (512, 544)], dst=t["krawT"], dt=F32)])]))
    scr("krT", [64, T], BF16)
    ph("kr", lambda: phase_rope(nc, "kr", t["krawT"], t["pos"], t["cst"], t["krT"], 1, T))
    scr("knT", [1536, T], BF16); scr("vv", [T, 1536], BF16)
    ph("ku", lambda: phase_normproj(nc, "ku", t["cT"], t["g_kv"], 512, T,
                                    [(t["w_uk"], fm_chunks(0, 1536, t["knT"], BF16)),
                                     (t["w_uv"], tm_chunks(0, 1536, t["vv"], BF16))]))
    scr("x1oT", [D, TO], F32); scr("hbT", [D, TO], BF16)
    ph("bn", lambda: phase_norm(nc, "bn", None, t["g_b_pre"], t["hbT"], D, TO, sel=(t["x1T"], t["sel"], t["x1oT"])))
    scr("cqT", [512, TO], F32); scr("zbT", [1536, TO], F32); scr("mqbT", [512, TO], BF16); scr("mzbT", [512, TO], F32)
    ph("bp", lambda: phase_proj(nc, "bp", t["hbT"], t["b_w_in"], D, TO,
                                fm_chunks(0, 512, t["cqT"], F32) + fm_chunks(512, 2048, t["zbT"], F32)
                                + fm_chunks(2048, 2560, t["mqbT"], BF16) + fm_chunks(2560, 3072, t["mzbT"], F32)))
    scr("qnT", [1536, TO], BF16); scr("qrawT", [1536, TO], F32)
    uq = []
    for h in range(12):
        b0 = h * 192
        uq += [dict(kind="fm", ranges=[(b0, b0 + 128)], dst=t["qnT"][h * 128:(h + 1) * 128, :], dt=BF16)]
        uq += [dict(kind="fm", ranges=[(b0 + 128, b0 + 192), (b0 + 160, b0 + 192), (b0 + 128, b0 + 160)],
                    dst=t["qrawT"][h * 128:(h + 1) * 128, :], dt=F32)]
    ph("bu", lambda: phase_normproj(nc, "bu", t["cqT"], t["g_bq"], 512, TO, [(t["b_w_uq"], uq)]))
    scr("qrT", [768, TO], BF16)
    ph("br", lambda: phase_rope(nc, "br", t["qrawT"], t["pos_own"], t["cst"], t["qrT"], 12, TO))
    scr("catbT", [D, TO], BF16)
    ph("ba", lambda: phase_attn(nc, "ba", t["knT"], t["krT"], t["vv"], t["qnT"], t["qrT"], t["zbT"], t["masks"], t["catbT"], 0,
                                12, T, TO, 192.0 ** -0.5, lambda g: 8 * g + 8))
    ph("bm", lambda: phase_attn(nc, "bm", t["memKT_b"], None, t["memV_b"], t["mqbT"], None, t["mzbT"], None, t["catbT"], 1536,
                                4, NM, TO, 128.0 ** -0.5, lambda g: 2))
    ph("bo", lambda: phase_outproj(nc, "bo", t["catbT"], t["b_w_out"], t["g_b_post"], t["x1oT"], outT, TO))

    for name, fn in phases:
        fn()
        if stop_after == name:
            break
    return nc


def _gain_layout(g):
    g = np.asarray(g, np.float32).reshape(-1)
    return np.ascontiguousarray(g.reshape(-1, 128).T)


def make_in_maps(inputs):
    f32 = np.float32
    x = np.asarray(inputs["x"], f32); mem = np.asarray(inputs["mem"], f32)
    pos = np.asarray(inputs["positions"], np.int32)
    shared = {
        "a_w_in": np.ascontiguousarray(inputs["a_w_in"][0], f32),
        "wg2a": np.ascontiguousarray(np.concatenate([inputs["a_w_g2"][0], inputs["a_b_g"][0][None, :]], 0), f32),
        "a_w_mem_kv": np.ascontiguousarray(inputs["a_w_mem_kv"][0], f32),
        "a_w_out": np.ascontiguousarray(inputs["a_w_out"][0], f32),
        "w_dkv": np.ascontiguousarray(inputs["w_dkv"], f32),
        "w_uk": np.ascontiguousarray(inputs["w_uk"], f32),
        "w_uv": np.ascontiguousarray(inputs["w_uv"], f32),
        "b_w_in": np.ascontiguousarray(inputs["b_w_in"][0], f32),
        "b_w_uq": np.ascontiguousarray(inputs["b_w_uq"][0], f32),
        "b_w_mem_kv": np.ascontiguousarray(inputs["b_w_mem_kv"][0], f32),
        "b_w_out": np.ascontiguousarray(inputs["b_w_out"][0], f32),
        "g_a_pre": _gain_layout(inputs["a_pre_norm"][0]), "g_a_mem": _gain_layout(inputs["a_mem_norm"][0]),
        "g_a_post": _gain_layout(inputs["a_post_norm"][0]), "g_kv_in": _gain_layout(inputs["kv_in_norm"]),
        "g_b_pre": _gain_layout(inputs["b_pre_norm"][0]), "g_b_mem": _gain_layout(inputs["b_mem_norm"][0]),
        "g_b_post": _gain_layout(inputs["b_post_norm"][0]), "g_kv": _gain_layout(inputs["kv_norm"]),
        "g_bq": _gain_layout(inputs["b_q_norm"][0]), "g_gla": _gain_layout(inputs["a_gla_norm"][0]),
    }
    i32 = np.arange(32, dtype=f32)
    freq = (np.float32(10000.0) ** (-(2 * i32) / np.float32(64))).astype(f32)
    cst = np.stack([np.concatenate([freq, freq]), np.concatenate([-np.ones(32, f32), np.ones(32, f32)])], 1).astype(f32)
    s_ = np.arange(64)[:, None]; t_ = np.arange(64)[None, :]
    tri = np.concatenate([np.where(s_ <= t_, -1.0 / 16.0, 0.0), np.where(s_ > t_, -1.0 / 16.0, 0.0),
                          np.where(s_ <= t_, 1.0, 0.0)], 1).astype(f32)
    shared["cst"] = np.ascontiguousarray(cst); shared["tri"] = np.ascontiguousarray(tri)
    maps = []
    for c in range(8):
        b, r = divmod(c, 2)
        m = dict(shared)
        m["xT"] = np.ascontiguousarray(x[b].T)
        m["memT"] = np.ascontiguousarray(mem[b].T)
        m["pos"] = np.ascontiguousarray(pos[b][None, :])
        own = pos[b].reshape(16, 2, 128)[:, r, :].reshape(1, TO)
        m["pos_own"] = np.ascontiguousarray(own)
        kk = np.arange(128)[:, None]; qq = np.arange(512)[None, :]
        msk = np.zeros((8, 128, 512), f32)
        for jj in range(8):
            keypos = jj * 128 + kk
            qpos = (2 * (qq // 128) + r) * 128 + (qq % 128)
            msk[jj] = (keypos <= qpos).astype(f32)
        m["masks"] = msk
        sel = np.zeros((128, 2), f32); sel[:, r] = 1.0
        m["sel"] = sel
        maps.append(m)
    return maps


_NC_CACHE = {}


def kernel(**inputs):
    maps = make_in_maps(inputs)
    if "nc" not in _NC_CACHE:
        _NC_CACHE["nc"] = build()
    nc = _NC_CACHE["nc"]
    res = run_bass_kernel_spmd(nc, maps, core_ids=list(range(8)))
    out = np.empty((4, T, D), np.float32)
    for c in range(8):
        b, r = divmod(c, 2)
        oT = np.asarray(res.results[c]["outT"], np.float32)
        o = oT.T.reshape(16, 128, D)
        out[b].reshape(16, 2, 128, D)[:, r] = o
    return out
```

```python
import numpy as np
from contextlib import ExitStack
import concourse.bass as bass
import concourse.mybir as mybir
from concourse.bass_utils import run_bass_kernel_spmd

F32 = mybir.dt.float32
BF16 = mybir.dt.bfloat16
I32 = mybir.dt.int32
AF = mybir.ActivationFunctionType
ALU = mybir.AluOpType

ENGS = ("sync", "scalar", "vector", "gpsimd", "tensor")

D = 2048
T = 4096
TO = 2048
NM = 256
EPS = 1e-6
DEBUG_OUT = []


class Buf:
    __slots__ = ("name", "w", "r", "dsem", "dcnt")

    def __init__(self, name):
        self.name = name
        self.w = None
        self.r = []
        self.dsem = None
        self.dcnt = 0


class Phase:
    def __init__(self, nc, name):
        self.nc = nc
        self.name = name
        self.stack = ExitStack()
        self.ops = {e: [] for e in ENGS}
        self.sems = []
        self.esem = {}
        self.ecnt = {}
        for e in ENGS:
            if e == "sync":
                continue
            self.esem[e] = nc.alloc_semaphore(name=f"{name}_{e}")
            self.sems.append(self.esem[e])
            self.ecnt[e] = 0
        self.emitted = {e: {} for e in ENGS}
        self.pending = {e: [] for e in ENGS}
        self.store_toks = []
        self.nbuf = 0
        self.nsem = 4
        self.rr = 0

    def sb(self, name, shape, dt):
        return self.stack.enter_context(self.nc.sbuf_tensor(f"{self.name}_{name}", list(shape), dt))

    def ps(self, name, shape, dt=F32):
        return self.stack.enter_context(self.nc.psum_tensor(f"{self.name}_{name}", list(shape), dt))

    def buf(self, name="b"):
        self.nbuf += 1
        return Buf(f"{name}{self.nbuf}")

    def bufs(self, n, name="b"):
        return [self.buf(name) for _ in range(n)]

    def _deps(self, eng, reads, writes):
        toks = []
        for b in reads:
            if b.w is not None:
                toks.append(b.w)
        for b in writes:
            if b.w is not None:
                toks.append(b.w)
            toks.extend(b.r)
        em = self.emitted[eng]
        best = {}
        for (sem, val) in toks:
            if isinstance(sem, str):
                if val != eng:
                    raise RuntimeError(f"cross-engine dep on unsignaled op ({val}->{eng}) in phase {self.name}")
                continue
            k = id(sem)
            if val > em.get(k, 0) and val > best.get(k, (None, 0))[1]:
                best[k] = (sem, val)
        waits = []
        for k, (sem, val) in best.items():
            em[k] = val
            waits.append((sem, val))
        return waits

    def _mark(self, tok, reads, writes):
        for b in reads:
            b.r.append(tok)
        for b in writes:
            b.w = tok
            b.r = []

    def op(self, eng, fn, reads=(), writes=(), signal=True):
        waits = self._deps(eng, reads, writes)
        if signal:
            self.ecnt[eng] += 1
            tok = (self.esem[eng], self.ecnt[eng])
            ptok = ("PENDING", eng)
            for (r_, w_) in self.pending[eng]:
                for b in w_:
                    if b.w == ptok:
                        b.w = tok
                for b in list(r_) + list(w_):
                    b.r = [tok if t == ptok else t for t in b.r]
            self.pending[eng] = []
            self._mark(tok, reads, writes)
            self.ops[eng].append((waits, fn, (self.esem[eng], 1)))
        else:
            ptok = ("PENDING", eng)
            self.pending[eng].append((list(reads), list(writes)))
            self._mark(ptok, reads, writes)
            self.ops[eng].append((waits, fn, None))

    def dma(self, q, out, in_, reads=(), writes=()):
        outs = out if isinstance(out, (list, tuple)) else [out]
        ins = in_ if isinstance(in_, (list, tuple)) else [in_]
        waits = self._deps(q, reads, writes)
        key = writes[0] if writes else reads[0]
        if key.dsem is None:
            key.dsem = self.nc.alloc_semaphore(name=f"{self.name}_d{self.nsem}")
            self.sems.append(key.dsem)
            self.nsem += 1
        for n, (o, i) in enumerate(zip(outs, ins)):
            key.dcnt += 16
            self.ops[q].append((waits if n == 0 else [],
                                (lambda e, o=o, i=i: e.dma_start(out=o, in_=i)), (key.dsem, 16)))
        tok = (key.dsem, key.dcnt)
        self._mark(tok, reads, writes)
        if not writes:
            self.store_toks.append(tok)
        return tok

    def evac_eng(self):
        self.rr += 1
        return "scalar" if self.rr % 2 else "vector"

    def finish(self):
        nc = self.nc
        last = {}
        for (sem, val) in self.store_toks:
            k = id(sem)
            if last.get(k, (None, 0))[1] < val:
                last[k] = (sem, val)
        final_waits = list(last.values())
        for e in ENGS:
            assert not self.pending[e], f"pending unsignaled ops at end of phase {self.name} on {e}"
        ops = self.ops
        with nc.Block(self.name) as block:
            def run(engname):
                def body(e):
                    for (waits, fn, inc) in ops[engname]:
                        for (sem, val) in waits:
                            e.wait_ge(sem, val)
                        ins = fn(e)
                        if inc is not None:
                            ins.then_inc(inc[0], inc[1])
                    if engname == "sync":
                        for (sem, val) in final_waits:
                            e.wait_ge(sem, val)
                return body
            block.sync(run("sync"))
            block.scalar(run("scalar"))
            block.vector(run("vector"))
            block.gpsimd(run("gpsimd"))
            block.tensor(run("tensor"))
        nc.clear_and_free_semaphores(self.sems)
        nc.all_engine_barrier()
        self.stack.close()


def copy_op(P, eng, out, in_, reads, writes):
    if eng == "scalar":
        P.op("scalar", lambda e: e.activation(out=out, in_=in_, func=AF.Copy), reads=reads, writes=writes)
    else:
        P.op(eng, lambda e: e.tensor_copy(out=out, in_=in_), reads=reads, writes=writes)


def phase_norm(nc, name, src, gain_d, dst, F, Tn, TT=512, sel=None):
    KC = F // 128
    P = Phase(nc, name)
    TT = min(TT, Tn)
    ones = P.sb("ones", [128, 128], BF16); b_ones = P.buf()
    P.op("vector", lambda e: e.memset(ones[:], 1.0), writes=[b_ones])
    g = P.sb("g", [128, KC], F32); b_g = P.buf()
    P.dma("sync", g[:], gain_d[:, :], writes=[b_g])
    NB = 2
    NBX = NB if sel is not None else 3
    xt = [P.sb(f"x{i}", [128, KC, TT], F32) for i in range(NBX)]; b_x = P.bufs(NBX)
    if sel is not None:
        xo = [P.sb(f"xo{i}", [128, KC, TT], F32) for i in range(NB)]; b_xo = P.bufs(NB)
        selt = P.sb("sel", [128, 2], F32); b_sel = P.buf()
        P.dma("sync", selt[:], sel[1][:, :], writes=[b_sel])
    ht = [P.sb(f"h{i}", [128, KC, TT], BF16) for i in range(NB)]; b_h = P.bufs(NB)
    sq = [P.sb(f"sq{i}", [128, TT], BF16) for i in range(3)]; b_sq = P.bufs(3)
    pss = [P.ps(f"ps{i}", [128, TT], F32) for i in range(2)]; b_ps = P.bufs(2)
    rs = [P.sb(f"rs{i}", [128, TT], F32) for i in range(2)]; b_rs = P.bufs(2)
    srcv = src.rearrange("(kc p) t -> p kc t", p=128) if sel is None else None
    dstv = dst.rearrange("(kc p) t -> p kc t", p=128)
    nsq = 0
    NT = Tn // TT

    def issue_load(it):
        t0 = it * TT
        i = it % NB
        ix = it % NBX
        x_ = xt[ix]
        if sel is None:
            P.dma("sync", x_[:], srcv[:, :, t0:t0 + TT], writes=[b_x[ix]])
        else:
            xv = sel[0].rearrange("(kc p) (i two q) -> p kc i two q", p=128, two=2, q=128)
            nb = TT // 128
            i0 = t0 // 128
            P.dma("sync", [x_[:, :, k * 128:(k + 1) * 128] for k in range(nb)], [xv[:, :, i0 + k, 0, :] for k in range(nb)], writes=[b_x[ix]])
            P.dma("sync", [xo[i][:, :, k * 128:(k + 1) * 128] for k in range(nb)], [xv[:, :, i0 + k, 1, :] for k in range(nb)], writes=[b_xo[i]])

    PF = NBX - 1
    for it in range(min(PF, NT)):
        issue_load(it)
    for it in range(NT):
        if it + PF < NT:
            issue_load(it + PF)
        t0 = it * TT
        i = it % NB
        ix = it % NBX
        x_ = xt[ix]
        if sel is not None:
            P.op("scalar", lambda e, x_=x_: e.activation(out=x_[:], in_=x_[:], func=AF.Copy, scale=selt[:, 0:1]), reads=[b_sel], writes=[b_x[ix]])
            P.op("vector", lambda e, x_=x_, xo_=xo[i]: e.scalar_tensor_tensor(out=x_[:], in0=xo_[:], scalar=selt[:, 1:2], in1=x_[:], op0=ALU.mult, op1=ALU.add),
                 reads=[b_sel, b_xo[i]], writes=[b_x[ix]])
            d32 = sel[2].rearrange("(kc p) t -> p kc t", p=128)
            P.dma("sync", d32[:, :, t0:t0 + TT], x_[:], reads=[b_x[ix]])
        ps = pss[it % 2]; bps = b_ps[it % 2]
        for kc in range(KC):
            j = nsq % 3; nsq += 1
            P.op("scalar", lambda e, kc=kc, j=j, x_=x_: e.activation(out=sq[j][:], in_=x_[:, kc, :], func=AF.Square),
                 reads=[b_x[ix]], writes=[b_sq[j]])
            P.op("tensor", lambda e, kc=kc, j=j, ps=ps: e.matmul(ps[:, :], lhsT=ones[:, :], rhs=sq[j][:], start=(kc == 0), stop=(kc == KC - 1)),
                 reads=[b_ones, b_sq[j]], writes=[bps], signal=True)
        r_ = rs[it % 2]; br = b_rs[it % 2]
        P.op("vector", lambda e, ps=ps, r_=r_: e.tensor_scalar(out=r_[:], in0=ps[:, :], scalar1=1.0 / F, scalar2=EPS, op0=ALU.mult, op1=ALU.add),
             reads=[bps], writes=[br])
        P.op("scalar", lambda e, r_=r_: e.activation(out=r_[:], in_=r_[:], func=AF.Sqrt), reads=[], writes=[br])
        P.op("vector", lambda e, r_=r_: e.reciprocal(out=r_[:], in_=r_[:]), reads=[], writes=[br])
        h_ = ht[i]
        for kc in range(KC):
            P.op("vector", lambda e, kc=kc, x_=x_, h_=h_, r_=r_: e.scalar_tensor_tensor(out=h_[:, kc, :], in0=x_[:, kc, :], scalar=g[:, kc:kc + 1], in1=r_[:], op0=ALU.mult, op1=ALU.mult),
                 reads=[b_x[ix], b_g, br], writes=[b_h[i]])
        P.dma("sync", dstv[:, :, t0:t0 + TT], h_[:], reads=[b_h[i]])
    P.finish()


def phase_normproj(nc, name, src, gain_d, F, Tn, jobs, TT=512):
    KC = F // 128
    P = Phase(nc, name)
    TT = min(TT, Tn)
    ones = P.sb("ones", [128, 128], BF16); b_ones = P.buf()
    P.op("vector", lambda e: e.memset(ones[:], 1.0), writes=[b_ones])
    g = P.sb("g", [128, KC], F32); b_g = P.buf()
    P.dma("sync", g[:], gain_d[:, :], writes=[b_g])
    wjobs = []
    for ji, (W, chunks) in enumerate(jobs):
        Wv = W.rearrange("(kc p) n -> p kc n", p=128)
        ncol = 0
        for ch in chunks:
            ch["n"] = sum(c1 - c0 for c0, c1 in ch["ranges"])
            ch["off"] = ncol
            ncol += ch["n"]
        w_ = P.sb(f"w{ji}", [128, KC, ncol], BF16); bw = P.buf()
        merged = []
        for ch in chunks:
            for (c0, c1) in ch["ranges"]:
                if merged and merged[-1][1] == c0:
                    merged[-1][1] = c1
                else:
                    merged.append([c0, c1])
        outs, ins, off = [], [], 0
        for (c0, c1) in merged:
            outs.append(w_[:, :, off:off + (c1 - c0)]); ins.append(Wv[:, :, c0:c1]); off += c1 - c0
        P.dma("gpsimd", outs, ins, writes=[bw])
        wjobs.append((w_, bw, chunks))
    NBX = 3 if KC * TT * 4 <= 32768 else 2
    xt = [P.sb(f"x{i}", [128, KC, TT], F32) for i in range(NBX)]; b_x = P.bufs(NBX)
    ht = [P.sb(f"h{i}", [128, KC, TT], BF16) for i in range(2)]; b_h = P.bufs(2)
    sq = [P.sb(f"sq{i}", [128, TT], BF16) for i in range(3)]; b_sq = P.bufs(3)
    pss = [P.ps(f"ps{i}", [128, TT], F32) for i in range(2)]; b_ps = P.bufs(2)
    rs = [P.sb(f"rs{i}", [128, TT], F32) for i in range(2)]; b_rs = P.bufs(2)
    NPP = 5
    ppj = [P.ps(f"pp{i}", [128, 512], F32) for i in range(NPP)]; b_pp = P.bufs(NPP)
    NST = 4
    st32 = [P.sb(f"s32_{i}", [128, 512], F32) for i in range(NST)]; b_s32 = P.bufs(NST)
    st16 = [P.sb(f"s16_{i}", [128, 512], BF16) for i in range(NST)]; b_s16 = P.bufs(NST)
    srcv = src.rearrange("(kc p) t -> p kc t", p=128)
    NT = Tn // TT
    cnt = dict(nsq=0, npp=0, nst=0)

    def issue_load(it):
        P.dma("sync", xt[it % NBX][:], srcv[:, :, it * TT:(it + 1) * TT], writes=[b_x[it % NBX]])

    PF = NBX - 1
    for it in range(min(PF, NT)):
        issue_load(it)
    def norm_tile(it):
        if it + PF < NT:
            issue_load(it + PF)
        t0 = it * TT
        i = it % 2
        ix = it % NBX
        x_ = xt[ix]
        ps = pss[it % 2]; bps = b_ps[it % 2]
        for kc in range(KC):
            j = cnt["nsq"] % 3; cnt["nsq"] += 1
            P.op("scalar", lambda e, kc=kc, j=j, x_=x_: e.activation(out=sq[j][:], in_=x_[:, kc, :], func=AF.Square),
                 reads=[b_x[ix]], writes=[b_sq[j]])
            P.op("tensor", lambda e, kc=kc, j=j, ps=ps: e.matmul(ps[:, :], lhsT=ones[:, :], rhs=sq[j][:], start=(kc == 0), stop=(kc == KC - 1)),
                 reads=[b_ones, b_sq[j]], writes=[bps], signal=True)
        r_ = rs[it % 2]; br = b_rs[it % 2]
        P.op("vector", lambda e, ps=ps, r_=r_: e.tensor_scalar(out=r_[:], in0=ps[:, :], scalar1=1.0 / F, scalar2=EPS, op0=ALU.mult, op1=ALU.add),
             reads=[bps], writes=[br])
        P.op("scalar", lambda e, r_=r_: e.activation(out=r_[:], in_=r_[:], func=AF.Sqrt), reads=[], writes=[br])
        P.op("vector", lambda e, r_=r_: e.reciprocal(out=r_[:], in_=r_[:]), reads=[], writes=[br])
        h_ = ht[i]
        for kc in range(KC):
            P.op("vector", lambda e, kc=kc, x_=x_, h_=h_, r_=r_: e.scalar_tensor_tensor(out=h_[:, kc, :], in0=x_[:, kc, :], scalar=g[:, kc:kc + 1], in1=r_[:], op0=ALU.mult, op1=ALU.mult),
                 reads=[b_x[ix], b_g, br], writes=[b_h[i]])

    def proj_tile(it):
        t0 = it * TT
        i = it % 2
        h_ = ht[i]
        for (w_, bw, chunks) in wjobs:
            for ch in chunks:
                n, o = ch["n"], ch["off"]
                subs = [(0, TT)] if ch["kind"] == "fm" else [(k * 128, 128) for k in range(TT // 128)]
                for (s0, sw) in subs:
                    pp = ppj[cnt["npp"] % NPP]; bpp = b_pp[cnt["npp"] % NPP]; cnt["npp"] += 1
                    for kc in range(KC):
                        if ch["kind"] == "fm":
                            P.op("tensor", lambda e, pp=pp, w_=w_, kc=kc, o=o, n=n, h_=h_: e.matmul(
                                pp[0:n, 0:TT], lhsT=w_[:, kc, o:o + n], rhs=h_[:, kc, :], start=(kc == 0), stop=(kc == KC - 1)),
                                reads=[bw, b_h[i]], writes=[bpp], signal=(kc == KC - 1))
                        else:
                            P.op("tensor", lambda e, pp=pp, w_=w_, kc=kc, o=o, n=n, h_=h_, s0=s0: e.matmul(
                                pp[:, 0:n], lhsT=h_[:, kc, s0:s0 + 128], rhs=w_[:, kc, o:o + n], start=(kc == 0), stop=(kc == KC - 1)),
                                reads=[bw, b_h[i]], writes=[bpp], signal=(kc == KC - 1))
                    k = cnt["nst"] % NST; cnt["nst"] += 1
                    if ch["dt"] == F32:
                        st, bst = st32[k], b_s32[k]
                    else:
                        st, bst = st16[k], b_s16[k]
                    if ch["kind"] == "fm":
                        copy_op(P, P.evac_eng(), st[0:n, 0:TT], pp[0:n, 0:TT], [bpp], [bst])
                        P.dma("sync", ch["dst"][:, t0:t0 + TT], st[0:n, 0:TT], reads=[bst])
                    else:
                        copy_op(P, P.evac_eng(), st[:, 0:n], pp[:, 0:n], [bpp], [bst])
                        P.dma("sync", ch["dst"][t0 + s0:t0 + s0 + 128, :], st[:, 0:n], reads=[bst])
    norm_tile(0)
    for it in range(NT):
        if it + 1 < NT:
            norm_tile(it + 1)
        proj_tile(it)
    P.finish()


def phase_proj(nc, name, srcT, W, K, Tn, chunks, fm_tt=512):
    KC = K // 128
    P = Phase(nc, name)
    fm_tt = min(fm_tt, Tn)
    hT = P.sb("hT", [128, KC, Tn], BF16)
    srcv = srcT.rearrange("(kc p) t -> p kc t", p=128)
    npiece = max(1, Tn // 512)
    pw = Tn // npiece
    b_hTs = P.bufs(npiece)
    for i in range(npiece):
        P.dma("sync", hT[:, :, i * pw:(i + 1) * pw], srcv[:, :, i * pw:(i + 1) * pw], writes=[b_hTs[i]])
    Wv = W.rearrange("(kc p) n -> p kc n", p=128)
    blocks = []
    cur = None
    for ch in chunks:
        n = sum(c1 - c0 for c0, c1 in ch["ranges"])
        ch["n"] = n
        if cur is None or cur["kind"] != ch["kind"] or cur["n"] + n > 512 or ch["kind"] == "tm":
            cur = dict(kind=ch["kind"], n=0, chunks=[])
            blocks.append(cur)
        ch["off"] = cur["n"]
        cur["n"] += n
        cur["chunks"].append(ch)
    NW = 2
    wsb = [P.sb(f"w{i}", [128, KC, 512], BF16) for i in range(NW)]; b_w = P.bufs(NW)
    NPS = 6
    pss = [P.ps(f"ps{i}", [128, 512], F32) for i in range(NPS)]; b_ps = P.bufs(NPS)
    NST = 4
    st32 = [P.sb(f"s32_{i}", [128, 512], F32) for i in range(NST)]; b_s32 = P.bufs(NST)
    st16 = [P.sb(f"s16_{i}", [128, 512], BF16) for i in range(NST)]; b_s16 = P.bufs(NST)
    nps = 0
    nst = 0
    for bi, blk in enumerate(blocks):
        w_ = wsb[bi % NW]; bw = b_w[bi % NW]
        outs, ins = [], []
        off = 0
        merged = []
        for ch in blk["chunks"]:
            for (c0, c1) in ch["ranges"]:
                if merged and merged[-1][1] == c0:
                    merged[-1][1] = c1
                else:
                    merged.append([c0, c1])
        for (c0, c1) in merged:
            outs.append(w_[:, :, off:off + (c1 - c0)])
            ins.append(Wv[:, :, c0:c1])
            off += c1 - c0
        P.dma("gpsimd", outs, ins, writes=[bw])
        if blk["kind"] == "fm":
            for it in range(Tn // fm_tt):
                t0 = it * fm_tt
                for ch in blk["chunks"]:
                    n, o = ch["n"], ch["off"]
                    ps = pss[nps % NPS]; bps = b_ps[nps % NPS]; nps += 1
                    for kc in range(KC):
                        P.op("tensor", lambda e, ps=ps, w_=w_, kc=kc, o=o, n=n, t0=t0: e.matmul(
                            ps[0:n, 0:fm_tt], lhsT=w_[:, kc, o:o + n], rhs=hT[:, kc, t0:t0 + fm_tt],
                            start=(kc == 0), stop=(kc == KC - 1)),
                            reads=[bw, b_hTs[t0 // pw]], writes=[bps], signal=(kc == KC - 1))
                    k = nst % NST; nst += 1
                    if ch["dt"] == F32:
                        st, bst = st32[k], b_s32[k]
                    else:
                        st, bst = st16[k], b_s16[k]
                    copy_op(P, P.evac_eng(), st[0:n, 0:fm_tt], ps[0:n, 0:fm_tt], [bps], [bst])
                    P.dma("sync", ch["dst"][:, t0:t0 + fm_tt], st[0:n, 0:fm_tt], reads=[bst])
        else:
            ch = blk["chunks"][0]
            n = ch["n"]
            for it in range(Tn // 128):
                t0 = it * 128
                ps = pss[nps % NPS]; bps = b_ps[nps % NPS]; nps += 1
                for kc in range(KC):
                    P.op("tensor", lambda e, ps=ps, w_=w_, kc=kc, n=n, t0=t0: e.matmul(
                        ps[:, 0:n], lhsT=hT[:, kc, t0:t0 + 128], rhs=w_[:, kc, 0:n],
                        start=(kc == 0), stop=(kc == KC - 1)),
                        reads=[bw, b_hTs[t0 // pw]], writes=[bps], signal=(kc == KC - 1))
                k = nst % NST; nst += 1
                if ch["dt"] == F32:
                    st, bst = st32[k], b_s32[k]
                else:
                    st, bst = st16[k], b_s16[k]
                copy_op(P, P.evac_eng(), st[:, 0:n], ps[:, 0:n], [bps], [bst])
                P.dma("sync", ch["dst"][t0:t0 + 128, :], st[:, 0:n], reads=[bst])
    P.finish()


def fm_chunks(c0, c1, dst, dt, row0=0):
    out = []
    c = c0
    while c < c1:
        n = min(128, c1 - c)
        out.append(dict(kind="fm", ranges=[(c, c + n)], dst=dst[row0 + (c - c0):row0 + (c - c0) + n, :], dt=dt))
        c += n
    return out


def tm_chunks(c0, c1, dst, dt, col0=0):
    out = []
    c = c0
    while c < c1:
        n = min(512, c1 - c)
        out.append(dict(kind="tm", ranges=[(c, c + n)], dst=dst[:, col0 + (c - c0):col0 + (c - c0) + n], dt=dt))
        c += n
    return out


def phase_gla(nc, name, qT, kT, ktm, vtm, glrT, zT, wg2a, tri, gg, catT, T=T):
    P = Phase(nc, name)
    H, DK, DV, C = 4, 192, 384, 64
    G_ = min(512, T)
    NG = T // G_
    CPG = G_ // C
    ones = P.sb("ones", [128, 128], BF16); b_ones = P.buf()
    P.op("vector", lambda e: e.memset(ones[:], 1.0), writes=[b_ones])
    tri_sb = P.sb("tri", [64, 192], F32); b_tri = P.buf()
    P.dma("sync", tri_sb[:], tri[:, :], writes=[b_tri])
    triC = tri_sb[:, 0:64]; triR = tri_sb[:, 64:128]; maskA = tri_sb[:, 128:192]
    wg = P.sb("wg", [17, 768], F32); b_wg = P.buf()
    P.dma("sync", wg[:], wg2a[:, :], writes=[b_wg])
    gl = P.sb("gl", [17, T], F32); b_gl = P.buf()
    P.op("vector", lambda e: e.memset(gl[:], 1.0), writes=[b_gl])
    P.dma("sync", gl[0:16, :], glrT[:, :], writes=[b_gl])
    gg_sb = P.sb("gg", [128, 3], F32); b_gg = P.buf()
    P.dma("sync", gg_sb[:], gg[:, :], writes=[b_gg])
    S = P.sb("S", [128, 8, DV], F32); b_S = P.buf()
    Sb = P.sb("Sb", [128, 8, DV], BF16); b_Sb = P.buf()
    P.op("vector", lambda e: e.memset(S[:], 0.0), writes=[b_S])
    P.op("gpsimd", lambda e: e.memset(Sb[:], 0.0), writes=[b_Sb])
    NGB = 2
    qg = [P.sb(f"qg{i}", [128, 8, G_], BF16) for i in range(NGB)]; b_qg = P.bufs(NGB)
    kg = [P.sb(f"kg{i}", [128, 8, G_], BF16) for i in range(NGB)]; b_kg = P.bufs(NGB)
    zg = [P.sb(f"zg{i}", [128, 12, G_], F32) for i in range(NGB)]; b_zg = P.bufs(NGB)
    cg = [P.sb(f"cg{i}", [128, 12, G_], BF16) for i in range(NGB)]; b_cg = P.bufs(NGB)
    NCB = 3
    kt = [P.sb(f"kt{i}", [64, 768], BF16) for i in range(NCB)]; b_kt = P.bufs(NCB)
    vt = [P.sb(f"vt{i}", [64, 1536], BF16) for i in range(NCB)]; b_vt = P.bufs(NCB)
    e1 = P.sb("e1", [64, 768], F32); b_e1 = P.buf()
    sp = P.sb("sp", [64, 768], F32); b_sp = P.buf()
    er = P.sb("er", [64, 768], F32); b_er = P.buf()
    kpp2 = [P.sb(f"kpp{i}", [64, 768], BF16) for i in range(2)]; b_kpp2 = P.bufs(2)
    eq2 = [P.sb(f"eq{i}", [128, 8, C], F32) for i in range(2)]; b_eq2 = P.bufs(2)
    ek = P.sb("ek", [128, 8, C], F32); b_ek = P.buf()
    qp2 = [P.sb(f"qp{i}", [128, 8, C], BF16) for i in range(2)]; b_qp2 = P.bufs(2)
    kp = P.sb("kp", [128, 8, C], BF16); b_kp = P.buf()
    am2 = [P.sb(f"am{i}", [64, 4, C], BF16) for i in range(2)]; b_am2 = P.bufs(2)
    osq = P.sb("osq", [128, 12, C], BF16); b_osq = P.buf()
    rstd = P.sb("rstd", [128, 4, C], F32); b_rstd = P.buf()
    sz = P.sb("sz", [128, 12, C], F32); b_sz = P.buf()
    on = P.sb("on", [128, 12, C], F32); b_on = P.buf()
    pP = [P.ps(f"pP{i}", [128, 512], F32) for i in range(2)]; b_pP = P.bufs(2)
    pC = P.ps("pC", [128, 8, C], F32); b_pC = P.buf()
    pA = P.ps("pA", [128, 4, C], F32); b_pA = P.buf()
    pO = [P.ps(f"pO{i}", [128, 6, C], F32) for i in range(2)]; b_pO = P.bufs(2)
    pS = [P.ps(f"pS{i}", [128, 512], F32) for i in range(2)]; b_pS = P.bufs(2)
    qTa = qT.rearrange("(h k) t -> k h t", k=DK)
    kTa = kT.rearrange("(h k) t -> k h t", k=DK)
    zTv = zT.rearrange("(s p) t -> p s t", p=128)
    catv = catT[0:1536, :].rearrange("(s p) t -> p s t", p=128)
    qscale = float(DK) ** -0.5
    NCH = T // C

    def Ga(c):
        g, cc = divmod(c, CPG)
        gi = g % NGB
        g0 = g * G_
        t0 = g0 + cc * C
        tc = cc * C
        ci = c % NCB
        d2 = c % 2
        kpp, b_kpp = kpp2[d2], b_kpp2[d2]
        eq, b_eq = eq2[d2], b_eq2[d2]
        qp, b_qp = qp2[d2], b_qp2[d2]
        am, b_am = am2[d2], b_am2[d2]
        kt_ = kt[ci]
        if cc == 0:
            P.dma("sync", [qg[gi][:, 0:4, :], qg[gi][0:64, 4:8, :]],
                  [qTa[0:128, :, g0:g0 + G_], qTa[128:192, :, g0:g0 + G_]], writes=[b_qg[gi]])
            P.dma("sync", [kg[gi][:, 0:4, :], kg[gi][0:64, 4:8, :]],
                  [kTa[0:128, :, g0:g0 + G_], kTa[128:192, :, g0:g0 + G_]], writes=[b_kg[gi]])
            P.dma("sync", zg[gi][:], zTv[:, :, g0:g0 + G_], writes=[b_zg[gi]])
        P.dma("sync", kt[ci][:], ktm[t0:t0 + C, :], writes=[b_kt[ci]])
        P.dma("sync", vt[ci][:], vtm[t0:t0 + C, :], writes=[b_vt[ci]])
        for hf in range(2):
            P.op("tensor", lambda e, hf=hf, t0=t0: e.matmul(pP[hf][0:64, 0:384], lhsT=gl[0:17, t0:t0 + C], rhs=wg[0:17, hf * 384:(hf + 1) * 384], start=True, stop=True),
                 reads=[b_gl, b_wg], writes=[b_pP[hf]])
        for hf in range(2):
            P.op("scalar", lambda e, hf=hf: e.activation(out=e1[:, hf * 384:(hf + 1) * 384], in_=pP[hf][0:64, 0:384], func=AF.Exp, scale=-1.0),
                 reads=[b_pP[hf]], writes=[b_e1])
        P.op("scalar", lambda e: e.activation(out=sp[:], in_=e1[:], func=AF.Ln, bias=1.0), reads=[b_e1], writes=[b_sp])

    def Gb(c):
        g, cc = divmod(c, CPG)
        gi = g % NGB
        g0 = g * G_
        t0 = g0 + cc * C
        tc = cc * C
        ci = c % NCB
        d2 = c % 2
        kpp, b_kpp = kpp2[d2], b_kpp2[d2]
        eq, b_eq = eq2[d2], b_eq2[d2]
        qp, b_qp = qp2[d2], b_qp2[d2]
        am, b_am = am2[d2], b_am2[d2]
        kt_ = kt[ci]
        for hf in range(2):
            P.op("tensor", lambda e, hf=hf: e.matmul(pP[hf][0:64, 0:384], lhsT=triR, rhs=sp[:, hf * 384:(hf + 1) * 384], start=True, stop=True),
                 reads=[b_tri, b_sp], writes=[b_pP[hf]])
        for hf in range(2):
            P.op("scalar", lambda e, hf=hf: e.activation(out=er[:, hf * 384:(hf + 1) * 384], in_=pP[hf][0:64, 0:384], func=AF.Exp),
                 reads=[b_pP[hf]], writes=[b_er])
        P.op("vector", lambda e, kt_=kt_: e.tensor_tensor(out=kpp[:], in0=kt_[:], in1=er[:], op=ALU.mult), reads=[b_kt[ci], b_er], writes=[b_kpp])
        for h in range(H):
            P.op("tensor", lambda e, h=h: e.matmul(pC[:, h, :], lhsT=sp[:, h * DK:h * DK + 128], rhs=triC, start=True, stop=True),
                 reads=[b_sp, b_tri], writes=[b_pC], signal=False)
        for h in range(H):
            P.op("tensor", lambda e, h=h: e.matmul(pC[0:64, 4 + h, :], lhsT=sp[:, h * DK + 128:(h + 1) * DK], rhs=triC, start=True, stop=True),
                 reads=[b_sp, b_tri], writes=[b_pC], signal=(h == H - 1))
        P.op("scalar", lambda e: e.activation(out=eq[:, 0:4, :], in_=pC[:, 0:4, :], func=AF.Exp), reads=[b_pC], writes=[b_eq])
        P.op("scalar", lambda e: e.activation(out=eq[0:64, 4:8, :], in_=pC[0:64, 4:8, :], func=AF.Exp), reads=[b_pC], writes=[b_eq])
        P.op("scalar", lambda e: e.activation(out=ek[:, 0:4, :], in_=pC[:, 0:4, :], func=AF.Exp, scale=-1.0), reads=[b_pC], writes=[b_ek])
        P.op("scalar", lambda e: e.activation(out=ek[0:64, 4:8, :], in_=pC[0:64, 4:8, :], func=AF.Exp, scale=-1.0), reads=[b_pC], writes=[b_ek])
        qg_, kg_ = qg[gi], kg[gi]
        P.op("vector", lambda e, qg_=qg_, tc=tc: e.scalar_tensor_tensor(out=qp[:, 0:4, :], in0=qg_[:, 0:4, tc:tc + C], scalar=qscale, in1=eq[:, 0:4, :], op0=ALU.mult, op1=ALU.mult),
             reads=[b_qg[gi], b_eq], writes=[b_qp])
        P.op("vector", lambda e, qg_=qg_, tc=tc: e.scalar_tensor_tensor(out=qp[0:64, 4:8, :], in0=qg_[0:64, 4:8, tc:tc + C], scalar=qscale, in1=eq[0:64, 4:8, :], op0=ALU.mult, op1=ALU.mult),
             reads=[b_qg[gi], b_eq], writes=[b_qp])
        P.op("gpsimd", lambda e, kg_=kg_, tc=tc: e.tensor_tensor(out=kp[:, 0:4, :], in0=kg_[:, 0:4, tc:tc + C], in1=ek[:, 0:4, :], op=ALU.mult),
             reads=[b_kg[gi], b_ek], writes=[b_kp])
        P.op("gpsimd", lambda e, kg_=kg_, tc=tc: e.tensor_tensor(out=kp[0:64, 4:8, :], in0=kg_[0:64, 4:8, tc:tc + C], in1=ek[0:64, 4:8, :], op=ALU.mult),
             reads=[b_kg[gi], b_ek], writes=[b_kp])

    def Gc(c):
        g, cc = divmod(c, CPG)
        gi = g % NGB
        g0 = g * G_
        t0 = g0 + cc * C
        tc = cc * C
        ci = c % NCB
        d2 = c % 2
        kpp, b_kpp = kpp2[d2], b_kpp2[d2]
        eq, b_eq = eq2[d2], b_eq2[d2]
        qp, b_qp = qp2[d2], b_qp2[d2]
        am, b_am = am2[d2], b_am2[d2]
        kt_ = kt[ci]
        for h in range(H):
            P.op("tensor", lambda e, h=h: e.matmul(pA[0:64, h, :], lhsT=kp[:, h, :], rhs=qp[:, h, :], start=True, stop=False),
                 reads=[b_kp, b_qp], writes=[b_pA], signal=False)
            P.op("tensor", lambda e, h=h: e.matmul(pA[0:64, h, :], lhsT=kp[0:64, 4 + h, :], rhs=qp[0:64, 4 + h, :], start=False, stop=True),
                 reads=[b_kp, b_qp], writes=[b_pA], signal=(h == H - 1))
        for h in range(H):
            P.op("vector", lambda e, h=h: e.tensor_tensor(out=am[:, h, :], in0=pA[0:64, h, :], in1=maskA, op=ALU.mult),
                 reads=[b_pA, b_tri], writes=[b_am])


    def Ua(c):
        g, cc = divmod(c, CPG)
        gi = g % NGB
        g0 = g * G_
        tc = cc * C
        ci = c % NCB
        d2 = c % 2
        kpp, b_kpp = kpp2[d2], b_kpp2[d2]
        eq, b_eq = eq2[d2], b_eq2[d2]
        qp, b_qp = qp2[d2], b_qp2[d2]
        am, b_am = am2[d2], b_am2[d2]
        vt_ = vt[ci]
        for h in range(H):
            po = pO[h // 2]; bpo = b_pO[h // 2]
            for j in range(3):
                sl = (h % 2) * 3 + j
                P.op("tensor", lambda e, h=h, j=j, po=po, sl=sl: e.matmul(po[:, sl, :], lhsT=Sb[:, h, j * 128:(j + 1) * 128], rhs=qp[:, h, :], start=True, stop=False),
                     reads=[b_Sb, b_qp], writes=[bpo], signal=False)
                P.op("tensor", lambda e, h=h, j=j, po=po, sl=sl: e.matmul(po[:, sl, :], lhsT=Sb[0:64, 4 + h, j * 128:(j + 1) * 128], rhs=qp[0:64, 4 + h, :], start=False, stop=False),
                     reads=[b_Sb, b_qp], writes=[bpo], signal=False)
                P.op("tensor", lambda e, h=h, j=j, po=po, sl=sl, vt_=vt_: e.matmul(po[:, sl, :], lhsT=vt_[:, h * DV + j * 128:h * DV + (j + 1) * 128], rhs=am[:, h, :], start=False, stop=True),
                     reads=[b_vt[ci], b_am], writes=[bpo], signal=(h % 2 == 1 and j == 2))

    def Ub(c):
        g, cc = divmod(c, CPG)
        gi = g % NGB
        g0 = g * G_
        tc = cc * C
        ci = c % NCB
        d2 = c % 2
        kpp, b_kpp = kpp2[d2], b_kpp2[d2]
        eq, b_eq = eq2[d2], b_eq2[d2]
        qp, b_qp = qp2[d2], b_qp2[d2]
        am, b_am = am2[d2], b_am2[d2]
        vt_ = vt[ci]
        for sl in range(8):
            h = sl % 4
            rows = 128 if sl < 4 else 64
            c0 = h * DK + (0 if sl < 4 else 128)
            ps_ = pS[sl % 2]; bps_ = b_pS[sl % 2]
            P.op("tensor", lambda e, ps_=ps_, rows=rows, c0=c0, h=h, vt_=vt_: e.matmul(ps_[0:rows, 0:DV], lhsT=kpp[:, c0:c0 + rows], rhs=vt_[:, h * DV:(h + 1) * DV], start=True, stop=True),
                 reads=[b_kpp, b_vt[ci]], writes=[bps_])
            P.op("vector", lambda e, ps_=ps_, rows=rows, sl=sl: e.scalar_tensor_tensor(out=S[0:rows, sl, :], in0=S[0:rows, sl, :], scalar=eq[0:rows, sl, C - 1:C], in1=ps_[0:rows, 0:DV], op0=ALU.mult, op1=ALU.add),
                 reads=[bps_, b_eq], writes=[b_S])
        P.op("scalar", lambda e: e.activation(out=Sb[:, 0:4, :], in_=S[:, 0:4, :], func=AF.Copy), reads=[b_S], writes=[b_Sb])
        P.op("gpsimd", lambda e: e.tensor_copy(out=Sb[0:64, 4:8, :], in_=S[0:64, 4:8, :]), reads=[b_S], writes=[b_Sb])

    def Uc(c):
        g, cc = divmod(c, CPG)
        gi = g % NGB
        g0 = g * G_
        tc = cc * C
        ci = c % NCB
        d2 = c % 2
        kpp, b_kpp = kpp2[d2], b_kpp2[d2]
        eq, b_eq = eq2[d2], b_eq2[d2]
        qp, b_qp = qp2[d2], b_qp2[d2]
        am, b_am = am2[d2], b_am2[d2]
        vt_ = vt[ci]
        for hp in range(2):
            P.op("scalar", lambda e, hp=hp: e.activation(out=osq[:, hp * 6:(hp + 1) * 6, :], in_=pO[hp][:, :, :], func=AF.Square),
                 reads=[b_pO[hp]], writes=[b_osq])
        for h in range(H):
            for j in range(3):
                P.op("tensor", lambda e, h=h, j=j: e.matmul(pC[:, h, :], lhsT=ones[:, :], rhs=osq[:, h * 3 + j, :], start=(j == 0), stop=(j == 2)),
                     reads=[b_ones, b_osq], writes=[b_pC], signal=(h == H - 1 and j == 2))
        P.op("vector", lambda e: e.tensor_scalar(out=rstd[:], in0=pC[:, 0:4, :], scalar1=1.0 / DV, scalar2=EPS, op0=ALU.mult, op1=ALU.add),
             reads=[b_pC], writes=[b_rstd])
        P.op("scalar", lambda e: e.activation(out=rstd[:], in_=rstd[:], func=AF.Sqrt), writes=[b_rstd])
        P.op("vector", lambda e: e.reciprocal(out=rstd[:], in_=rstd[:]), writes=[b_rstd])
        zg_ = zg[gi]
        P.op("scalar", lambda e, zg_=zg_, tc=tc: e.activation(out=sz[:], in_=zg_[:, :, tc:tc + C], func=AF.Silu), reads=[b_zg[gi]], writes=[b_sz])
        for hp in range(2):
            for j in range(3):
                P.op("vector", lambda e, hp=hp, j=j: e.scalar_tensor_tensor(
                    out=on[:, hp * 6 + j:hp * 6 + 6:3, :], in0=pO[hp][:, j:6:3, :], scalar=gg_sb[:, j:j + 1],
                    in1=rstd[:, hp * 2:hp * 2 + 2, :], op0=ALU.mult, op1=ALU.mult),
                    reads=[b_pO[hp], b_gg, b_rstd], writes=[b_on])
        cg_ = cg[gi]
        P.op("gpsimd", lambda e, cg_=cg_, tc=tc: e.tensor_tensor(out=cg_[:, :, tc:tc + C], in0=on[:], in1=sz[:], op=ALU.mult),
             reads=[b_on, b_sz], writes=[b_cg[gi]])
        if cc == CPG - 1:
            P.dma("sync", catv[:, :, g0:g0 + G_], cg[gi][:], reads=[b_cg[gi]])


    Ga(0); Gb(0); Gc(0)
    for c in range(NCH):
        nxt = c + 1 < NCH
        if nxt:
            Ga(c + 1)
        Ua(c)
        if nxt:
            Gb(c + 1)
        Ub(c)
        if nxt:
            Gc(c + 1)
        Uc(c)
    P.finish()

def phase_attn(nc, name, KTn, KTr, V, QTn, QTr, zT, masks, catT, row0, H, Tk, Tq, scale, nkb):
    P = Phase(nc, name)
    QG = min(512, Tq)
    NQG = Tq // QG
    NKB = Tk // 128
    HG = 4
    ones = P.sb("ones", [128, 128], BF16); b_ones = P.buf()
    P.op("vector", lambda e: e.memset(ones[:], 1.0), writes=[b_ones])
    if KTr is not None:
        kr = P.sb("kr", [64, Tk], BF16); b_kr = P.buf()
        P.dma("sync", kr[:], KTr[:, :], writes=[b_kr])
    if masks is not None:
        mk = P.sb("mk", [128, 8, 512], F32); b_mk = P.buf()
        P.dma("sync", mk[:], masks.rearrange("j p q -> p j q"), writes=[b_mk])
    vg = [P.sb(f"vg{i}", [128, NKB, HG * 128], BF16) for i in range(2)]; b_vg = P.bufs(2)
    kh = [P.sb(f"kh{i}", [128, Tk], BF16) for i in range(2)]; b_kh = P.bufs(2)
    NQ = 4
    qn = [P.sb(f"qn{i}", [128, QG], BF16) for i in range(NQ)]; b_qn = P.bufs(NQ)
    qr = [P.sb(f"qr{i}", [64, QG], BF16) for i in range(NQ)]; b_qr = P.bufs(NQ)
    zt = [P.sb(f"zt{i}", [128, QG], F32) for i in range(NQ)]; b_zt = P.bufs(NQ)
    NP = 4
    pt = [P.sb(f"pt{i}", [128, QG], BF16) for i in range(NP)]; b_pt = P.bufs(NP)
    pf = [P.sb(f"pf{i}", [128, QG], F32) for i in range(2)]; b_pf = P.bufs(2)
    NE = 2
    rden = [P.sb(f"rden{i}", [128, QG], F32) for i in range(NE)]; b_rden = P.bufs(NE)
    sz = [P.sb(f"sz{i}", [128, QG], F32) for i in range(NE)]; b_sz = P.bufs(NE)
    t1 = [P.sb(f"t1{i}", [128, QG], F32) for i in range(NE)]; b_t1 = P.bufs(NE)
    co = [P.sb(f"co{i}", [128, QG], BF16) for i in range(4)]; b_co = P.bufs(4)
    NS = 4
    pS = [P.ps(f"pS{i}", [128, 512], F32) for i in range(NS)]; b_pS = P.bufs(NS)
    pO = [P.ps(f"pO{i}", [128, 512], F32) for i in range(2)]; b_pO = P.bufs(2)
    pD = [P.ps(f"pD{i}", [128, 512], F32) for i in range(2)]; b_pD = P.bufs(2)
    Vv = V.rearrange("(kb p) n -> p kb n", p=128)
    pairs = [(h, g) for h in range(H) for g in range(NQG)]
    items = []
    for pi, (h, g) in enumerate(pairs):
        nk = nkb(g)
        for j in range(nk):
            items.append((pi, h, g, j, nk))

    def load_head(h):
        if h >= H:
            return
        hg, hh = divmod(h, HG)
        if hh == 0:
            P.dma("sync", vg[hg % 2][:], Vv[:, :, hg * HG * 128:(hg + 1) * HG * 128], writes=[b_vg[hg % 2]])
        P.dma("sync", kh[h % 2][:], KTn[h * 128:(h + 1) * 128, :], writes=[b_kh[h % 2]])

    def load_pair(pi):
        if pi >= len(pairs):
            return
        h, g = pairs[pi]
        qi = pi % NQ
        q0 = g * QG
        P.dma("sync", qn[qi][:], QTn[h * 128:(h + 1) * 128, q0:q0 + QG], writes=[b_qn[qi]])
        if QTr is not None:
            P.dma("sync", qr[qi][:], QTr[h * 64:(h + 1) * 64, q0:q0 + QG], writes=[b_qr[qi]])
        P.dma("sync", zt[qi][:], zT[h * 128:(h + 1) * 128, q0:q0 + QG], writes=[b_zt[qi]])

    slots = {}
    cnt = dict(nit=0, npf=0)

    def score(idx):
        pi, h, g, j, nk = items[idx]
        if j == 0:
            if g == 0:
                load_head(h + 1)
            load_pair(pi + 2)
        qi = pi % NQ
        k_ = kh[h % 2]; bk = b_kh[h % 2]
        nit = cnt["nit"]; cnt["nit"] += 1
        ps = pS[nit % NS]; bps = b_pS[nit % NS]
        p_ = pt[nit % NP]; bp = b_pt[nit % NP]
        slots[idx] = (p_, bp)
        P.op("tensor", lambda e, ps=ps, k_=k_, j=j, qi=qi: e.matmul(ps[:, 0:QG], lhsT=k_[:, j * 128:(j + 1) * 128], rhs=qn[qi][:], start=True, stop=(QTr is None)),
             reads=[bk, b_qn[qi]], writes=[bps], signal=(QTr is None))
        if QTr is not None:
            P.op("tensor", lambda e, ps=ps, j=j, qi=qi: e.matmul(ps[:, 0:QG], lhsT=kr[:, j * 128:(j + 1) * 128], rhs=qr[qi][:], start=False, stop=True),
                 reads=[b_kr, b_qr[qi]], writes=[bps])
        jj = j - (nk - 8)
        if masks is not None and jj >= 0:
            npf = cnt["npf"]; cnt["npf"] += 1
            f_ = pf[npf % 2]; bf_ = b_pf[npf % 2]
            P.op("scalar", lambda e, ps=ps, f_=f_: e.activation(out=f_[:], in_=ps[:, 0:QG], func=AF.Exp, scale=scale), reads=[bps], writes=[bf_])
            P.op("vector", lambda e, f_=f_, p_=p_, jj=jj: e.tensor_tensor(out=p_[:], in0=f_[:], in1=mk[:, jj, :], op=ALU.mult), reads=[bf_, b_mk], writes=[bp])
        else:
            P.op("scalar", lambda e, ps=ps, p_=p_: e.activation(out=p_[:], in_=ps[:, 0:QG], func=AF.Exp, scale=scale), reads=[bps], writes=[bp])

    def accum(idx):
        pi, h, g, j, nk = items[idx]
        hg, hh = divmod(h, HG)
        v_ = vg[hg % 2]; bv = b_vg[hg % 2]
        po = pO[pi % 2]; bpo = b_pO[pi % 2]
        pd = pD[pi % 2]; bpd = b_pD[pi % 2]
        p_, bp = slots.pop(idx)
        P.op("tensor", lambda e, po=po, v_=v_, j=j, hh=hh, p_=p_, nk=nk: e.matmul(po[:, 0:QG], lhsT=v_[:, j, hh * 128:(hh + 1) * 128], rhs=p_[:], start=(j == 0), stop=(j == nk - 1)),
             reads=[bv, bp], writes=[bpo], signal=False)
        P.op("tensor", lambda e, pd=pd, p_=p_, j=j, nk=nk: e.matmul(pd[:, 0:QG], lhsT=ones[:, :], rhs=p_[:], start=(j == 0), stop=(j == nk - 1)),
             reads=[b_ones, bp], writes=[bpd], signal=True)
        if j == nk - 1:
            qi = pi % NQ
            ei = pi % NE
            ci = pi % 4
            q0 = g * QG
            P.op("vector", lambda e, pd=pd, ei=ei: e.reciprocal(out=rden[ei][:], in_=pd[:, 0:QG]), reads=[bpd], writes=[b_rden[ei]])
            P.op("scalar", lambda e, qi=qi, ei=ei: e.activation(out=sz[ei][:], in_=zt[qi][:], func=AF.Silu), reads=[b_zt[qi]], writes=[b_sz[ei]])
            P.op("vector", lambda e, po=po, ei=ei: e.tensor_tensor(out=t1[ei][:], in0=po[:, 0:QG], in1=rden[ei][:], op=ALU.mult), reads=[bpo, b_rden[ei]], writes=[b_t1[ei]])
            P.op("gpsimd", lambda e, ci=ci, ei=ei: e.tensor_tensor(out=co[ci][:], in0=t1[ei][:], in1=sz[ei][:], op=ALU.mult), reads=[b_t1[ei], b_sz[ei]], writes=[b_co[ci]])
            P.dma("sync", catT[row0 + h * 128:row0 + (h + 1) * 128, q0:q0 + QG], co[ci][:], reads=[b_co[ci]])

    load_head(0)
    load_pair(0)
    load_pair(1)
    LA = 2
    n = len(items)
    for idx in range(min(LA, n)):
        score(idx)
    for idx in range(n):
        if idx + LA < n:
            score(idx + LA)
        accum(idx)
    P.finish()


def phase_outproj(nc, name, catT, W, gain_d, resT, outT, Tn):
    P = Phase(nc, name)
    KC = 16
    TT = 512
    ones = P.sb("ones", [128, 128], BF16); b_ones = P.buf()
    P.op("vector", lambda e: e.memset(ones[:], 1.0), writes=[b_ones])
    g = P.sb("g", [128, KC], F32); b_g = P.buf()
    P.dma("sync", g[:], gain_d[:, :], writes=[b_g])
    w = P.sb("w", [128, KC, D], BF16); b_w = [P.buf() for _ in range(4)]
    Wv = W.rearrange("(kc p) n -> p kc n", p=128)
    for i in range(4):
        P.dma("gpsimd", w[:, :, i * 512:(i + 1) * 512], Wv[:, :, i * 512:(i + 1) * 512], writes=[b_w[i]])
    ct = [P.sb(f"c{i}", [128, KC, TT], BF16) for i in range(2)]; b_ct = P.bufs(2)
    xr = P.sb("xr", [128, KC, TT], F32); b_xr = P.bufs(4)
    y = P.sb("y", [128, KC, TT], F32); b_y = P.bufs(KC)
    sq = [P.sb(f"sq{i}", [128, TT], BF16) for i in range(3)]; b_sq = P.bufs(3)
    rss = [P.sb(f"rs{i}", [128, TT], F32) for i in range(2)]; b_rss = P.bufs(2)
    NPS = 5
    pss = [P.ps(f"ps{i}", [128, 512], F32) for i in range(NPS)]; b_ps = P.bufs(NPS)
    pn = [P.ps(f"pn{i}", [128, 512], F32) for i in range(2)]; b_pn = P.bufs(2)
    catv = catT.rearrange("(kc p) t -> p kc t", p=128)
    resv = resT.rearrange("(kc p) t -> p kc t", p=128)
    outv = outT.rearrange("(kc p) t -> p kc t", p=128)
    nps = 0
    nsq = 0
    for it in range(Tn // TT):
        t0 = it * TT
        i = it % 2
        rs = rss[i]; b_rs = b_rss[i]
        if it == 0:
            P.dma("sync", ct[0][:], catv[:, :, 0:TT], writes=[b_ct[0]])
        if it + 1 < Tn // TT:
            P.dma("sync", ct[(it + 1) % 2][:], catv[:, :, t0 + TT:t0 + 2 * TT], writes=[b_ct[(it + 1) % 2]])
        for q in range(4):
            P.dma("sync", xr[:, q * 4:(q + 1) * 4, :], resv[:, q * 4:(q + 1) * 4, t0:t0 + TT], writes=[b_xr[q]])
        pn_ = pn[it % 2]; bpn = b_pn[it % 2]
        pend = []

        def stat_mm(j, n, pn_=pn_, bpn=bpn):
            P.op("tensor", lambda e, pn_=pn_, j=j, n=n: e.matmul(pn_[:, 0:TT], lhsT=ones[:, :], rhs=sq[j][:], start=(n == 0), stop=(n == KC - 1)),
                 reads=[b_ones, b_sq[j]], writes=[bpn], signal=True)

        for n in range(KC):
            ps = pss[nps % NPS]; bps = b_ps[nps % NPS]; nps += 1
            for kc in range(KC):
                P.op("tensor", lambda e, ps=ps, kc=kc, n=n, i=i: e.matmul(ps[:, 0:TT], lhsT=w[:, kc, n * 128:(n + 1) * 128], rhs=ct[i][:, kc, :], start=(kc == 0), stop=(kc == KC - 1)),
                     reads=[b_w[n // 4], b_ct[i]], writes=[bps], signal=(kc == KC - 1))
            P.op("scalar", lambda e, ps=ps, n=n: e.activation(out=y[:, n, :], in_=ps[:, 0:TT], func=AF.Copy), reads=[bps], writes=[b_y[n]])
            j = nsq % 3; nsq += 1
            P.op("scalar", lambda e, n=n, j=j: e.activation(out=sq[j][:], in_=y[:, n, :], func=AF.Square), reads=[b_y[n]], writes=[b_sq[j]])
            pend.append((j, n))
            if len(pend) > 1:
                stat_mm(*pend.pop(0))
        while pend:
            stat_mm(*pend.pop(0))
        P.op("vector", lambda e, pn_=pn_, rs=rs: e.tensor_scalar(out=rs[:], in0=pn_[:, 0:TT], scalar1=1.0 / D, scalar2=EPS, op0=ALU.mult, op1=ALU.add), reads=[bpn], writes=[b_rs])
        P.op("scalar", lambda e, rs=rs: e.activation(out=rs[:], in_=rs[:], func=AF.Sqrt), writes=[b_rs])
        P.op("vector", lambda e, rs=rs: e.reciprocal(out=rs[:], in_=rs[:]), writes=[b_rs])
        for n in range(KC):
            P.op("vector", lambda e, n=n, rs=rs: e.scalar_tensor_tensor(out=y[:, n, :], in0=y[:, n, :], scalar=g[:, n:n + 1], in1=rs[:], op0=ALU.mult, op1=ALU.mult),
                 reads=[b_g, b_rs], writes=[b_y[n]])
            P.op("vector", lambda e, n=n: e.tensor_tensor(out=xr[:, n, :], in0=xr[:, n, :], in1=y[:, n, :], op=ALU.add),
                 reads=[b_y[n]], writes=[b_xr[n // 4]])
            if n % 4 == 3:
                q = n // 4
                P.dma("sync", outv[:, q * 4:(q + 1) * 4, t0:t0 + TT], xr[:, q * 4:(q + 1) * 4, :], reads=[b_xr[q]])
    P.finish()


def phase_rope(nc, name, raw, pos, cst, out, NH, Tn):
    P = Phase(nc, name)
    TW = 1024
    c = P.sb("c", [64, 2], F32); b_c = P.buf()
    P.dma("sync", c[:], cst[:, :], writes=[b_c])
    hp = P.sb("hp", [64, 1], F32); b_hp = P.buf()
    P.op("vector", lambda e: e.memset(hp[:], float(np.pi / 2)), writes=[b_hp])
    cs = P.sb("cs", [64, Tn], F32); b_cs = P.buf()
    sn = P.sb("sn", [64, Tn], F32); b_sn = P.buf()
    pi_ = P.sb("pi", [64, TW], I32); b_pi = P.buf()
    a0 = P.sb("a0", [64, TW], F32); b_a0 = P.buf()
    a1 = P.sb("a1", [64, TW], F32); b_a1 = P.buf()
    ki = P.sb("ki", [64, TW], I32); b_ki = P.buf()
    kf = P.sb("kf", [64, TW], F32); b_kf = P.buf()
    f1 = P.sb("f1", [64, TW], F32); b_f1 = P.buf()
    f2 = P.sb("f2", [64, TW], F32); b_f2 = P.buf()
    C1 = 6.28125
    C2 = float(2 * np.pi - 6.28125)
    PI = float(np.pi)
    for it in range(Tn // TW):
        t0 = it * TW
        P.dma("sync", pi_[:], pos[:, t0:t0 + TW].partition_broadcast(64), writes=[b_pi])
        P.op("vector", lambda e: e.tensor_copy(out=a0[:], in_=pi_[:]), reads=[b_pi], writes=[b_a0])
        P.op("vector", lambda e: e.tensor_scalar(out=a0[:], in0=a0[:], scalar1=c[:, 0:1], scalar2=None, op0=ALU.mult), reads=[b_c], writes=[b_a0])
        P.op("vector", lambda e: e.tensor_scalar(out=a1[:], in0=a0[:], scalar1=float(1 / (2 * np.pi)), scalar2=None, op0=ALU.mult), reads=[b_a0], writes=[b_a1])
        P.op("vector", lambda e: e.tensor_copy(out=ki[:], in_=a1[:]), reads=[b_a1], writes=[b_ki])
        P.op("vector", lambda e: e.tensor_copy(out=kf[:], in_=ki[:]), reads=[b_ki], writes=[b_kf])
        P.op("vector", lambda e: e.scalar_tensor_tensor(out=a1[:], in0=kf[:], scalar=-C1, in1=a0[:], op0=ALU.mult, op1=ALU.add), reads=[b_kf, b_a0], writes=[b_a1])
        P.op("vector", lambda e: e.scalar_tensor_tensor(out=a0[:], in0=kf[:], scalar=-C2, in1=a1[:], op0=ALU.mult, op1=ALU.add), reads=[b_kf, b_a1], writes=[b_a0])
        P.op("vector", lambda e: e.tensor_scalar(out=f1[:], in0=a0[:], scalar1=PI, scalar2=-2 * PI, op0=ALU.is_gt, op1=ALU.mult), reads=[b_a0], writes=[b_f1])
        P.op("vector", lambda e: e.tensor_scalar(out=f2[:], in0=a0[:], scalar1=-PI, scalar2=2 * PI, op0=ALU.is_lt, op1=ALU.mult), reads=[b_a0], writes=[b_f2])
        P.op("vector", lambda e: e.tensor_tensor(out=f1[:], in0=f1[:], in1=f2[:], op=ALU.add), reads=[b_f2], writes=[b_f1])
        P.op("vector", lambda e: e.tensor_tensor(out=a1[:], in0=a0[:], in1=f1[:], op=ALU.add), reads=[b_a0, b_f1], writes=[b_a1])
        P.op("scalar", lambda e, t0=t0: e.activation(out=sn[:, t0:t0 + TW], in_=a1[:], func=AF.Sin), reads=[b_a1], writes=[b_sn])
        P.op("scalar", lambda e: e.activation(out=f2[:], in_=a1[:], func=AF.Abs), reads=[b_a1], writes=[b_f2])
        P.op("scalar", lambda e, t0=t0: e.activation(out=cs[:, t0:t0 + TW], in_=f2[:], func=AF.Sin, scale=-1.0, bias=hp[:]), reads=[b_f2, b_hp], writes=[b_cs])
        P.op("vector", lambda e, t0=t0: e.tensor_scalar(out=sn[:, t0:t0 + TW], in0=sn[:, t0:t0 + TW], scalar1=c[:, 1:2], scalar2=None, op0=ALU.mult), reads=[b_c], writes=[b_sn])
    xa = [P.sb(f"xa{i}", [64, Tn], F32) for i in range(2)]; b_xa = P.bufs(2)
    xb = [P.sb(f"xb{i}", [64, Tn], F32) for i in range(2)]; b_xb = P.bufs(2)
    ob = [P.sb(f"ob{i}", [64, Tn], BF16) for i in range(2)]; b_ob = P.bufs(2)
    def ld(h):
        i = h % 2
        P.dma("sync", xa[i][:], raw[h * 128:h * 128 + 64, :], writes=[b_xa[i]])
        P.dma("sync", xb[i][:], raw[h * 128 + 64:h * 128 + 128, :], writes=[b_xb[i]])

    ld(0)
    for h in range(NH):
        i = h % 2
        if h + 1 < NH:
            ld(h + 1)
        P.op("vector", lambda e, i=i: e.tensor_tensor(out=xa[i][:], in0=xa[i][:], in1=cs[:], op=ALU.mult), reads=[b_cs], writes=[b_xa[i]])
        P.op("gpsimd", lambda e, i=i: e.tensor_tensor(out=xb[i][:], in0=xb[i][:], in1=sn[:], op=ALU.mult), reads=[b_sn], writes=[b_xb[i]])
        P.op("vector", lambda e, i=i: e.tensor_tensor(out=ob[i][:], in0=xa[i][:], in1=xb[i][:], op=ALU.add), reads=[b_xa[i], b_xb[i]], writes=[b_ob[i]])
        P.dma("sync", out[h * 64:(h + 1) * 64, :], ob[i][:], reads=[b_ob[i]])
    P.finish()


GAINS = ["a_pre", "a_mem", "a_post", "kv_in", "b_pre", "b_mem", "b_post"]


def build(stop_after=None, debug=()):
    nc = bass.Bass("TRN2", target_bir_lowering=False)
    t = {}

    def inp(name, shape, dt=F32):
        t[name] = nc.dram_tensor(name, list(shape), dt, kind="ExternalInput").ap()
        return t[name]

    def scr(name, shape, dt):
        kind = "ExternalOutput" if name in debug else "Internal"
        t[name] = nc.dram_tensor(name, list(shape), dt, kind=kind).ap()
        return t[name]

    inp("xT", [D, T]); inp("memT", [D, NM])
    inp("pos", [1, T], I32); inp("pos_own", [1, TO], I32)
    inp("masks", [8, 128, 512]); inp("sel", [128, 2]); inp("cst", [64, 2]); inp("tri", [64, 192])
    inp("a_w_in", [D, 5648]); inp("wg2a", [17, 768]); inp("a_w_mem_kv", [D, 1024]); inp("a_w_out", [D, D])
    inp("w_dkv", [D, 576]); inp("w_uk", [512, 1536]); inp("w_uv", [512, 1536])
    inp("b_w_in", [D, 3072]); inp("b_w_uq", [512, 2304]); inp("b_w_mem_kv", [D, 1024]); inp("b_w_out", [D, D])
    for gname in GAINS:
        inp("g_" + gname, [128, 16])
    inp("g_kv", [128, 4]); inp("g_bq", [128, 4]); inp("g_gla", [128, 3])
    outT = nc.dram_tensor("outT", [D, TO], F32, kind="ExternalOutput").ap()

    phases = []

    def ph(name, fn):
        phases.append((name, fn))

    for L in ("a", "b"):
        scr(f"memKT_{L}", [512, NM], BF16); scr(f"memV_{L}", [NM, 512], BF16)
        ph(f"mp{L}", lambda L=L: phase_normproj(nc, f"mp{L}", t["memT"], t[f"g_{L}_mem"], D, NM,
                                                 [(t[f"{L}_w_mem_kv"], fm_chunks(0, 512, t[f"memKT_{L}"], BF16)
                                                   + tm_chunks(512, 1024, t[f"memV_{L}"], BF16))]))
    scr("hT", [D, T], BF16)
    ph("an", lambda: phase_norm(nc, "an", t["xT"], t["g_a_pre"], t["hT"], D, T))
    scr("qT", [768, T], BF16); scr("kT", [768, T], BF16); scr("glrT", [16, T], F32); scr("zT", [1536, T], F32)
    scr("mqT", [512, T], BF16); scr("mzT", [512, T], F32); scr("ktm", [T, 768], BF16); scr("vtm", [T, 1536], BF16)
    ph("ap", lambda: phase_proj(nc, "ap", t["hT"], t["a_w_in"], D, T,
                                fm_chunks(0, 768, t["qT"], BF16) + fm_chunks(768, 1536, t["kT"], BF16)
                                + fm_chunks(3072, 3088, t["glrT"], F32) + fm_chunks(3088, 4624, t["zT"], F32)
                                + fm_chunks(4624, 5136, t["mqT"], BF16) + fm_chunks(5136, 5648, t["mzT"], F32)
                                + tm_chunks(768, 1536, t["ktm"], BF16) + tm_chunks(1536, 3072, t["vtm"], BF16)))
    scr("catT", [D, T], BF16)
    ph("ag", lambda: phase_gla(nc, "ag", t["qT"], t["kT"], t["ktm"], t["vtm"], t["glrT"], t["zT"], t["wg2a"], t["tri"], t["g_gla"], t["catT"]))
    ph("am", lambda: phase_attn(nc, "am", t["memKT_a"], None, t["memV_a"], t["mqT"], None, t["mzT"], None, t["catT"], 1536,
                                4, NM, T, 128.0 ** -0.5, lambda g: 2))
    scr("x1T", [D, T], F32)
    ph("ao", lambda: phase_outproj(nc, "ao", t["catT"], t["a_w_out"], t["g_a_post"], t["xT"], t["x1T"], T))
    scr("cT", [512, T], F32); scr("krawT", [128, T], F32)
    ph("kp", lambda: phase_normproj(nc, "kp", t["x1T"], t["g_kv_in"], D, T,
                                    [(t["w_dkv"], fm_chunks(0, 512, t["cT"], F32)
                                      + [dict(kind="fm", ranges=[(512, 576), (544, 576), (512, 544)], dst=t["krawT"], dt=F32)])]))
    scr("krT", [64, T], BF16)
    ph("kr", lambda: phase_rope(nc, "kr", t["krawT"], t["pos"], t["cst"], t["krT"], 1, T))
    scr("knT", [1536, T], BF16); scr("vv", [T, 1536], BF16)
    ph("ku", lambda: phase_normproj(nc, "ku", t["cT"], t["g_kv"], 512, T,
                                    [(t["w_uk"], fm_chunks(0, 1536, t["knT"], BF16)),
                                     (t["w_uv"], tm_chunks(0, 1536, t["vv"], BF16))]))
    scr("x1oT", [D, TO], F32); scr("hbT", [D, TO], BF16)
    ph("bn", lambda: phase_norm(nc, "bn", None, t["g_b_pre"], t["hbT"], D, TO, sel=(t["x1T"], t["sel"], t["x1oT"])))
    scr("cqT", [512, TO], F32); scr("zbT", [1536, TO], F32); scr("mqbT", [512, TO], BF16); scr("mzbT", [512, TO], F32)
    ph("bp", lambda: phase_proj(nc, "bp", t["hbT"], t["b_w_in"], D, TO,
                                fm_chunks(0, 512, t["cqT"], F32) + fm_chunks(512, 2048, t["zbT"], F32)
                                + fm_chunks(2048, 2560, t["mqbT"], BF16) + fm_chunks(2560, 3072, t["mzbT"], F32)))
    scr("qnT", [1536, TO], BF16); scr("qrawT", [1536, TO], F32)
    uq = []
    for h in range(12):
        b0 = h * 192
        uq += [dict(kind="fm", ranges=[(b0, b0 + 128)], dst=t["qnT"][h * 128:(h + 1) * 128, :], dt=BF16)]
        uq += [dict(kind="fm", ranges=[(b0 + 128, b0 + 192), (b0 + 160, b0 + 192), (b0 + 128, b0 + 160)],
                    dst=t["qrawT"][h * 128:(h + 1) * 128, :], dt=F32)]
    ph("bu", lambda: phase_normproj(nc, "bu", t["cqT"], t["g_bq"], 512, TO, [(t["b_w_uq"], uq)]))
    scr("qrT", [768, TO], BF16)
    ph("br", lambda: phase_rope(nc, "br", t["qrawT"], t["pos_own"], t["cst"], t["qrT"], 12, TO))
    scr("catbT", [D, TO], BF16)
    ph("ba", lambda: phase_attn(nc, "ba", t["knT"], t["krT"], t["vv"], t["qnT"], t["qrT"], t["zbT"], t["masks"], t["catbT"], 0,
                                12, T, TO, 192.0 ** -0.5, lambda g: 8 * g + 8))
    ph("bm", lambda: phase_attn(nc, "bm", t["memKT_b"], None, t["memV_b"], t["mqbT"], None, t["mzbT"], None, t["catbT"], 1536,
                                4, NM, TO, 128.0 ** -0.5, lambda g: 2))
    ph("bo", lambda: phase_outproj(nc, "bo", t["catbT"], t["b_w_out"], t["g_b_post"], t["x1oT"], outT, TO))

    for name, fn in phases:
        fn()
        if stop_after == name:
            break
    return nc


def _gain_layout(g):
    g = np.asarray(g, np.float32).reshape(-1)
    return np.ascontiguousarray(g.reshape(-1, 128).T)


def make_in_maps(inputs):
    f32 = np.float32
    x = np.asarray(inputs["x"], f32); mem = np.asarray(inputs["mem"], f32)
    pos = np.asarray(inputs["positions"], np.int32)
    shared = {
        "a_w_in": np.ascontiguousarray(inputs["a_w_in"][0], f32),
        "wg2a": np.ascontiguousarray(np.concatenate([inputs["a_w_g2"][0], inputs["a_b_g"][0][None, :]], 0), f32),
        "a_w_mem_kv": np.ascontiguousarray(inputs["a_w_mem_kv"][0], f32),
        "a_w_out": np.ascontiguousarray(inputs["a_w_out"][0], f32),
        "w_dkv": np.ascontiguousarray(inputs["w_dkv"], f32),
        "w_uk": np.ascontiguousarray(inputs["w_uk"], f32),
        "w_uv": np.ascontiguousarray(inputs["w_uv"], f32),
        "b_w_in": np.ascontiguousarray(inputs["b_w_in"][0], f32),
        "b_w_uq": np.ascontiguousarray(inputs["b_w_uq"][0], f32),
        "b_w_mem_kv": np.ascontiguousarray(inputs["b_w_mem_kv"][0], f32),
        "b_w_out": np.ascontiguousarray(inputs["b_w_out"][0], f32),
        "g_a_pre": _gain_layout(inputs["a_pre_norm"][0]), "g_a_mem": _gain_layout(inputs["a_mem_norm"][0]),
        "g_a_post": _gain_layout(inputs["a_post_norm"][0]), "g_kv_in": _gain_layout(inputs["kv_in_norm"]),
        "g_b_pre": _gain_layout(inputs["b_pre_norm"][0]), "g_b_mem": _gain_layout(inputs["b_mem_norm"][0]),
        "g_b_post": _gain_layout(inputs["b_post_norm"][0]), "g_kv": _gain_layout(inputs["kv_norm"]),
        "g_bq": _gain_layout(inputs["b_q_norm"][0]), "g_gla": _gain_layout(inputs["a_gla_norm"][0]),
    }
    i32 = np.arange(32, dtype=f32)
    freq = (np.float32(10000.0) ** (-(2 * i32) / np.float32(64))).astype(f32)
    cst = np.stack([np.concatenate([freq, freq]), np.concatenate([-np.ones(32, f32), np.ones(32, f32)])], 1).astype(f32)
    s_ = np.arange(64)[:, None]; t_ = np.arange(64)[None, :]
    tri = np.concatenate([np.where(s_ <= t_, -1.0 / 16.0, 0.0), np.where(s_ > t_, -1.0 / 16.0, 0.0),
                          np.where(s_ <= t_, 1.0, 0.0)], 1).astype(f32)
    shared["cst"] = np.ascontiguousarray(cst); shared["tri"] = np.ascontiguousarray(tri)
    maps = []
    for c in range(8):
        b, r = divmod(c, 2)
        m = dict(shared)
        m["xT"] = np.ascontiguousarray(x[b].T)
        m["memT"] = np.ascontiguousarray(mem[b].T)
        m["pos"] = np.ascontiguousarray(pos[b][None, :])
        own = pos[b].reshape(16, 2, 128)[:, r, :].reshape(1, TO)
        m["pos_own"] = np.ascontiguousarray(own)
        kk = np.arange(128)[:, None]; qq = np.arange(512)[None, :]
        msk = np.zeros((8, 128, 512), f32)
        for jj in range(8):
            keypos = jj * 128 + kk
            qpos = (2 * (qq // 128) + r) * 128 + (qq % 128)
            msk[jj] = (keypos <= qpos).astype(f32)
        m["masks"] = msk
        sel = np.zeros((128, 2), f32); sel[:, r] = 1.0
        m["sel"] = sel
        maps.append(m)
    return maps


_NC_CACHE = {}


def kernel(**inputs):
    maps = make_in_maps(inputs)
    if "nc" not in _NC_CACHE:
        _NC_CACHE["nc"] = build()
    nc = _NC_CACHE["nc"]
    res = run_bass_kernel_spmd(nc, maps, core_ids=list(range(8)))
    out = np.empty((4, T, D), np.float32)
    for c in range(8):
        b, r = divmod(c, 2)
        oT = np.asarray(res.results[c]["outT"], np.float32)
        o = oT.T.reshape(16, 128, D)
        out[b].reshape(16, 2, 128, D)[:, r] = o
    return out
```
